# Optimizing a Trainium2 kernel written in Bass

```python
import math
import jax, jax.numpy as jnp
from jax import lax
import numpy as np

D_MODEL = 1024
BATCH = 2
SEQ = 8192
DEPTH = 2
DEC_BATCH = 8
DEC_SEQ = 32
PAST_LEN = 4096

CHUNK = 64
N_EVEN = (DEPTH + 1) // 2
N_ODD = DEPTH // 2
PLE_DIM = 256
D_FF = -(-8 * D_MODEL // (3 * 256)) * 256
DEEPNORM_ALPHA = (2 * DEPTH) ** 0.25
DEEPNORM_BETA = (8 * DEPTH) ** -0.25

POOL_WIDTH = D_MODEL // 2
POOL_WINDOWS = (2, 4, 8, 16)
POOL_GROUP = POOL_WIDTH // len(POOL_WINDOWS)
POOL_HIST = max(POOL_WINDOWS) - 1
CONV_WIDTH = D_MODEL // 2
CONV_K = 3
EVEN_PROJ = POOL_WIDTH + 3 * CONV_WIDTH
SGU_WIDTH = D_MODEL // 2
SGU_BLOCK = 128
SGU_GROUPS = 4
SGU_GDIM = SGU_WIDTH // SGU_GROUPS
SSM_INNER = D_MODEL // 2
SSM_HEADDIM = 64
SSM_HEADS = SSM_INNER // SSM_HEADDIM
SSM_GROUPS = 2
SSM_STATE = 128
SSM_CONV_K = 4
SSM_XBC = SSM_INNER + 2 * SSM_GROUPS * SSM_STATE
ODD_PROJ = 2 * SGU_WIDTH + SSM_INNER + SSM_XBC + SSM_HEADS
MIX_WIDTH = D_MODEL

kernel_name = 'hybrid_pool_conv_sgu_ssd_stream_step'


def layer_norm(x, g, b, eps=1e-5):
    xf = x.astype(jnp.float32)
    mu = jnp.mean(xf, -1, keepdims=True)
    var = jnp.mean(jnp.square(xf - mu), -1, keepdims=True)
    return ((xf - mu) * lax.rsqrt(var + eps) * g + b).astype(x.dtype)


def causal_dw_conv(hist, x, w):
    K, L = w.shape[0], x.shape[1]
    xp = jnp.concatenate([hist.astype(x.dtype), x], axis=1)
    y = w[0] * xp[:, 0:L]
    for k in range(1, K):
        y = y + w[k] * xp[:, k:k + L]
    return y, xp[:, L:]


def pool_mixer(hist, u, pos0, w_grp, scale):
    b, L, c = u.shape
    up = jnp.concatenate([hist.astype(u.dtype), u], axis=1)
    cs = jnp.pad(jnp.cumsum(up.astype(jnp.float32), axis=1), ((0, 0), (1, 0), (0, 0)))
    pos = pos0 + jnp.arange(L)
    uf = u.astype(jnp.float32)
    outs = []
    for gi, win in enumerate(POOL_WINDOWS):
        sl = slice(gi * POOL_GROUP, (gi + 1) * POOL_GROUP)
        end = cs[:, POOL_HIST + 1:POOL_HIST + 1 + L, sl]
        start = cs[:, POOL_HIST + 1 - win:POOL_HIST + 1 - win + L, sl]
        cnt = jnp.minimum(win, pos + 1).astype(jnp.float32)[None, :, None]
        outs.append((end - start) / cnt - uf[..., sl])
    d = jnp.stack(outs, axis=2)
    y = jnp.einsum('blgc,gcd->blgd', d, w_grp.astype(jnp.float32)).reshape(b, L, c) * scale
    return y.astype(u.dtype), up[:, L:]


def short_conv_mixer(hist, b_gate, c_gate, h, w_conv):
    cv, new_hist = causal_dw_conv(hist, c_gate * h, w_conv)
    return b_gate * cv, new_hist


def sgu_mixer(u, v, w_s, b_s, ln_g, ln_b):
    v = layer_norm(v, ln_g, ln_b)
    b, L, _ = u.shape
    blk = min(SGU_BLOCK, L)
    nb = L // blk
    mask = jnp.tril(jnp.ones((blk, blk), bool))
    ws = jnp.where(mask, w_s[:, :blk, :blk], 0.0)
    vb = v.reshape(b, nb, blk, SGU_GROUPS, SGU_GDIM)
    mixed = jnp.einsum('gts,bnsgd->bntgd', ws, vb) + b_s[:, :blk].T[None, None, :, :, None]
    return u * mixed.reshape(b, L, SGU_WIDTH), v


def ssd_scan(x, dt, a, bm, cm, h0):
    f32 = jnp.float32
    b, L, H, P = x.shape
    G, N = bm.shape[2], bm.shape[3]
    R = H // G
    Q = min(CHUNK, L)
    nc = L // Q
    xc = x.astype(f32).reshape(b, nc, Q, G, R, P)
    dtc = dt.astype(f32).reshape(b, nc, Q, G, R)
    bc = bm.astype(f32).reshape(b, nc, Q, G, N)
    cc = cm.astype(f32).reshape(b, nc, Q, G, N)
    acum = jnp.cumsum(dtc * a.reshape(G, R), axis=2)
    causal = jnp.tril(jnp.ones((Q, Q), bool))[:, :, None, None]
    seg = acum[:, :, :, None] - acum[:, :, None, :]
    decay = jnp.exp(jnp.where(causal, seg, -jnp.inf))
    cb = jnp.einsum('bctgn,bcsgn->bctsg', cc, bc)
    m = cb[..., None] * decay * dtc[:, :, None]
    y = jnp.einsum('bctsgr,bcsgrp->bctgrp', m, xc)
    to_end = jnp.exp(acum[:, :, -1:] - acum) * dtc
    s_blk = jnp.einsum('bcsgn,bcsgrp->bcgrpn', bc, xc * to_end[..., None])
    blk_decay = jnp.exp(acum[:, :, -1])

    def step(h, inp):
        s, dcy = inp
        return h * dcy[..., None, None] + s, h

    h_last, h_in = lax.scan(step, h0.astype(f32).reshape(b, G, R, P, N),
                            (jnp.moveaxis(s_blk, 1, 0), jnp.moveaxis(blk_decay, 1, 0)))
    h_in = jnp.moveaxis(h_in, 0, 1)
    y = y + jnp.einsum('bctgn,bcgrpn->bctgrp', cc, h_in) * jnp.exp(acum)[..., None]
    return y.reshape(b, L, H, P), h_last.reshape(b, H, P, N)


def gated_rmsnorm(y, z, w, eps=1e-5):
    g = y * jax.nn.silu(z.astype(jnp.float32))
    gs = g.reshape(*g.shape[:-1], SSM_GROUPS, SSM_INNER // SSM_GROUPS)
    gs = gs * lax.rsqrt(jnp.mean(jnp.square(gs), -1, keepdims=True) + eps)
    return gs.reshape(g.shape) * w


def mamba2_mixer(hist, h0, z, xbc, dt_raw, conv_w, conv_b, dt_bias, a_log, d_skip, norm_w):
    b, L, _ = z.shape
    xc, new_hist = causal_dw_conv(hist, xbc, conv_w)
    xc = jax.nn.silu(xc + conv_b)
    gn = SSM_GROUPS * SSM_STATE
    xs = xc[..., :SSM_INNER].reshape(b, L, SSM_HEADS, SSM_HEADDIM)
    bm = xc[..., SSM_INNER:SSM_INNER + gn].reshape(b, L, SSM_GROUPS, SSM_STATE)
    cm = xc[..., SSM_INNER + gn:].reshape(b, L, SSM_GROUPS, SSM_STATE)
    dt = jax.nn.softplus(dt_raw.astype(jnp.float32) + dt_bias.astype(jnp.float32))
    a = -jnp.exp(a_log.astype(jnp.float32))
    y, h_last = ssd_scan(xs, dt, a, bm, cm, h0)
    y = y + d_skip.astype(jnp.float32)[:, None] * xs.astype(jnp.float32)
    y = gated_rmsnorm(y.reshape(b, L, SSM_INNER), z, norm_w)
    return y.astype(z.dtype), new_hist, h_last.astype(h0.dtype)


def swiglu(x, w1, w3, w2):
    return (jax.nn.silu(x @ w1) * (x @ w3)) @ w2


def trunk(x, ple, pos0, st_pool, st_conv, st_ssm_conv, st_ssm, w):
    new_pool, new_conv, new_v, new_ssm_conv, new_ssm = [], [], [], [], []
    for i in range(DEPTH):
        j = i // 2
        if i % 2 == 0:
            hcat = x @ w['w_in_even'][j]
            a_in = hcat[..., :POOL_WIDTH]
            b_gate = hcat[..., POOL_WIDTH:POOL_WIDTH + CONV_WIDTH]
            c_gate = hcat[..., POOL_WIDTH + CONV_WIDTH:POOL_WIDTH + 2 * CONV_WIDTH]
            h_in = hcat[..., POOL_WIDTH + 2 * CONV_WIDTH:]
            ya, hp = pool_mixer(st_pool[j], a_in, pos0, w['pool_mix_w'][j], w['pool_scale'][j])
            yb, hc = short_conv_mixer(st_conv[j], b_gate, c_gate, h_in, w['conv_w'][j])
            mix = jnp.concatenate([ya, yb], -1) @ w['w_out_even'][j]
            new_pool.append(hp)
            new_conv.append(hc)
        else:
            hcat = x @ w['w_in_odd'][j]
            o1 = 2 * SGU_WIDTH
            o2 = o1 + SSM_INNER
            o3 = o2 + SSM_XBC
            u = jax.nn.gelu(hcat[..., :SGU_WIDTH])
            v = jax.nn.gelu(hcat[..., SGU_WIDTH:o1])
            yc, v_n = sgu_mixer(u, v, w['sgu_w'][j], w['sgu_b'][j], w['sgu_ln_g'][j], w['sgu_ln_b'][j])
            yd, hc, hs = mamba2_mixer(st_ssm_conv[j], st_ssm[j], hcat[..., o1:o2], hcat[..., o2:o3],
                                      hcat[..., o3:], w['ssm_conv_w'][j], w['ssm_conv_b'][j],
                                      w['ssm_dt_bias'][j], w['ssm_a_log'][j], w['ssm_d'][j],
                                      w['ssm_norm_w'][j])
            mix = jnp.concatenate([yc, yd], -1) @ w['w_out_odd'][j]
            new_v.append(v_n)
            new_ssm_conv.append(hc)
            new_ssm.append(hs)
        x = layer_norm(DEEPNORM_ALPHA * x + mix, w['ln1_g'][i], w['ln1_b'][i])
        x = layer_norm(DEEPNORM_ALPHA * x + swiglu(x, w['w_ff1'][i], w['w_ff3'][i], w['w_ff2'][i]),
                       w['ln2_g'][i], w['ln2_b'][i])
        x = x + (ple[i] @ w['w_ple'][i]) * jax.nn.sigmoid(x @ w['w_ple_gate'][i])
    return (x, jnp.stack(new_pool), jnp.stack(new_conv), jnp.stack(new_v),
            jnp.stack(new_ssm_conv), jnp.stack(new_ssm))


def setup_inputs(seed: int = 0) -> dict:
    key = jax.random.key(seed)
    ks = iter(jax.random.split(key, 64))
    f32 = jnp.float32
    NE, NO = N_EVEN, N_ODD

    def nrm(shape, scale):
        return jax.random.normal(next(ks), shape, f32) * scale

    def gain(shape):
        return 1.0 + nrm(shape, 0.02)

    dt0 = jnp.exp(jax.random.uniform(next(ks), (NO, SSM_HEADS), f32, math.log(1e-3), math.log(1e-1)))
    a0 = jax.random.uniform(next(ks), (NO, SSM_HEADS), f32, 1.0, 16.0)
    return {
        'x_prompt': nrm((BATCH, SEQ, D_MODEL), 1.0),
        'x_sample': nrm((DEC_BATCH, DEC_SEQ, D_MODEL), 1.0),
        'p_prompt': nrm((DEPTH, BATCH, SEQ, PLE_DIM), 1.0),
        'p_sample': nrm((DEPTH, DEC_BATCH, DEC_SEQ, PLE_DIM), 1.0),
        'state_pool': nrm((NE, DEC_BATCH, POOL_HIST, POOL_WIDTH), 1.0),
        'state_conv': nrm((NE, DEC_BATCH, CONV_K - 1, CONV_WIDTH), 1.0),
        'state_ssm_conv': nrm((NO, DEC_BATCH, SSM_CONV_K - 1, SSM_XBC), 1.0),
        'state_ssm': nrm((NO, DEC_BATCH, SSM_HEADS, SSM_HEADDIM, SSM_STATE), 0.5),
        'w_in_even': nrm((NE, D_MODEL, EVEN_PROJ), D_MODEL ** -0.5),
        'pool_mix_w': nrm((NE, len(POOL_WINDOWS), POOL_GROUP, POOL_GROUP), POOL_GROUP ** -0.5),
        'pool_scale': 1.0 + nrm((NE, POOL_WIDTH), 0.1),
        'conv_w': nrm((NE, CONV_K, CONV_WIDTH), CONV_K ** -0.5),
        'w_out_even': nrm((NE, MIX_WIDTH, D_MODEL), MIX_WIDTH ** -0.5 * DEEPNORM_BETA),
        'w_in_odd': nrm((NO, D_MODEL, ODD_PROJ), D_MODEL ** -0.5),
        'sgu_w': nrm((NO, SGU_GROUPS, SGU_BLOCK, SGU_BLOCK), SGU_BLOCK ** -0.5),
        'sgu_b': gain((NO, SGU_GROUPS, SGU_BLOCK)),
        'sgu_ln_g': gain((NO, SGU_WIDTH)),
        'sgu_ln_b': nrm((NO, SGU_WIDTH), 0.02),
        'ssm_conv_w': nrm((NO, SSM_CONV_K, SSM_XBC), SSM_CONV_K ** -0.5),
        'ssm_conv_b': nrm((NO, SSM_XBC), 0.02),
        'ssm_dt_bias': dt0 + jnp.log(-jnp.expm1(-dt0)),
        'ssm_a_log': jnp.log(a0),
        'ssm_d': gain((NO, SSM_HEADS)),
        'ssm_norm_w': gain((NO, SSM_INNER)),
        'w_out_odd': nrm((NO, MIX_WIDTH, D_MODEL), MIX_WIDTH ** -0.5 * DEEPNORM_BETA),
        'ln1_g': gain((DEPTH, D_MODEL)),
        'ln1_b': nrm((DEPTH, D_MODEL), 0.02),
        'ln2_g': gain((DEPTH, D_MODEL)),
        'ln2_b': nrm((DEPTH, D_MODEL), 0.02),
        'w_ff1': nrm((DEPTH, D_MODEL, D_FF), D_MODEL ** -0.5),
        'w_ff3': nrm((DEPTH, D_MODEL, D_FF), D_MODEL ** -0.5),
        'w_ff2': nrm((DEPTH, D_FF, D_MODEL), D_FF ** -0.5 * DEEPNORM_BETA),
        'w_ple': nrm((DEPTH, PLE_DIM, D_MODEL), PLE_DIM ** -0.5),
        'w_ple_gate': nrm((DEPTH, D_MODEL, D_MODEL), D_MODEL ** -0.5),
    }


def reference(x_prompt, x_sample, p_prompt, p_sample, state_pool, state_conv, state_ssm_conv, state_ssm,
              w_in_even, pool_mix_w, pool_scale, conv_w, w_out_even,
              w_in_odd, sgu_w, sgu_b, sgu_ln_g, sgu_ln_b, ssm_conv_w, ssm_conv_b, ssm_dt_bias,
              ssm_a_log, ssm_d, ssm_norm_w, w_out_odd,
              ln1_g, ln1_b, ln2_g, ln2_b, w_ff1, w_ff3, w_ff2, w_ple, w_ple_gate):
    w = {
        'w_in_even': w_in_even, 'pool_mix_w': pool_mix_w, 'pool_scale': pool_scale,
        'conv_w': conv_w, 'w_out_even': w_out_even,
        'w_in_odd': w_in_odd, 'sgu_w': sgu_w, 'sgu_b': sgu_b, 'sgu_ln_g': sgu_ln_g, 'sgu_ln_b': sgu_ln_b,
        'ssm_conv_w': ssm_conv_w, 'ssm_conv_b': ssm_conv_b, 'ssm_dt_bias': ssm_dt_bias,
        'ssm_a_log': ssm_a_log, 'ssm_d': ssm_d, 'ssm_norm_w': ssm_norm_w, 'w_out_odd': w_out_odd,
        'ln1_g': ln1_g, 'ln1_b': ln1_b, 'ln2_g': ln2_g, 'ln2_b': ln2_b,
        'w_ff1': w_ff1, 'w_ff3': w_ff3, 'w_ff2': w_ff2, 'w_ple': w_ple, 'w_ple_gate': w_ple_gate,
    }
    bp = x_prompt.shape[0]
    dtp = x_prompt.dtype
    z_pool = jnp.zeros((N_EVEN, bp, POOL_HIST, POOL_WIDTH), dtp)
    z_conv = jnp.zeros((N_EVEN, bp, CONV_K - 1, CONV_WIDTH), dtp)
    z_ssm_conv = jnp.zeros((N_ODD, bp, SSM_CONV_K - 1, SSM_XBC), dtp)
    z_ssm = jnp.zeros((N_ODD, bp, SSM_HEADS, SSM_HEADDIM, SSM_STATE), state_ssm.dtype)
    y_prompt, pool_p, conv_p, _, ssm_conv_p, ssm_p = trunk(
        x_prompt, p_prompt, 0, z_pool, z_conv, z_ssm_conv, z_ssm, w)
    y_sample, pool_s, conv_s, sgu_v_s, ssm_conv_s, ssm_s = trunk(
        x_sample, p_sample, PAST_LEN, state_pool, state_conv, state_ssm_conv, state_ssm, w)
    return (y_prompt, y_sample, pool_p, pool_s, conv_p, conv_s, sgu_v_s, ssm_conv_p, ssm_conv_s, ssm_p, ssm_s)
```

```python
from contextlib import ExitStack
import os
import numpy as np
import concourse.bass as bass
import concourse.mybir as mybir
from concourse.bass_utils import run_bass_kernel_spmd

F32 = mybir.dt.float32
BF16 = mybir.dt.bfloat16
AF = mybir.ActivationFunctionType
ALU = mybir.AluOpType

D = 1024
SEQ = 8192
DSEQ = 32
DFF = 2816
NJ = DFF // 128
ALPHA = 4.0 ** 0.25
EPS = 1e-5
NEG = -30000.0
QMAX = int(os.environ.get('MK_Q', '128'))
STRICT = os.environ.get('MK_STRICT') == '1'


class Op:
    __slots__ = ("eng", "fn", "reads", "writes", "key", "dsem", "sig", "ticket",
                 "waits", "deps", "dcount", "dinc")


class Prog:
    CENG = ("pe", "act", "dve", "pool")

    def __init__(self, nc):
        self.nc = nc
        self.ops = []
        self.ctr = 0

    def add(self, eng, fn, reads=(), writes=(), dsem=None, after=None, dinc=16):
        op = Op()
        op.eng, op.fn = eng, fn
        op.reads, op.writes = list(reads), list(writes)
        op.dsem, op.dinc = dsem, dinc
        self.ctr += 1
        if after is None:
            op.key = (self.ctr, 0)
        elif after == "start":
            op.key = (0, self.ctr)
        else:
            op.key = (after.key[0], after.key[1] + self.ctr)
        op.sig, op.ticket, op.waits, op.dcount = False, None, [], None
        self.ops.append(op)
        return op

    def finalize(self):
        ops = sorted(self.ops, key=lambda o: o.key)
        self.ops = ops
        last_w, readers, dcount = {}, {}, {}
        for op in ops:
            deps = {}
            for t in op.reads:
                w = last_w.get(t)
                if w is not None:
                    deps[id(w)] = w
            for t in op.writes:
                w = last_w.get(t)
                if w is not None:
                    deps[id(w)] = w
                for r in readers.get(t, ()):
                    deps[id(r)] = r
            deps.pop(id(op), None)
            need = []
            rset = set(op.reads)
            for p in deps.values():
                if p.dsem is not None:
                    need.append(p)
                elif p.eng == op.eng and op.dsem is None:
                    if op.eng == "pe":
                        continue
                    if STRICT or (rset & set(p.writes)):
                        need.append(p)
                else:
                    need.append(p)
            op.deps = need
            for p in need:
                if p.dsem is None:
                    p.sig = True
            for t in op.reads:
                readers.setdefault(t, []).append(op)
            for t in op.writes:
                last_w[t] = op
                readers[t] = []
            if op.dsem is not None:
                dcount[op.dsem] = dcount.get(op.dsem, 0) + op.dinc
                op.dcount = dcount[op.dsem]
        self.dtotal = dcount
        tick = {e: 0 for e in self.CENG}
        for op in ops:
            if op.dsem is None and op.sig:
                tick[op.eng] += 1
                op.ticket = tick[op.eng]
        seen = {e: {} for e in ("pe", "act", "dve", "pool", "sp")}
        for op in ops:
            w = {}
            for p in op.deps:
                if p.dsem is not None:
                    k, v = ("d", p.dsem), p.dcount
                else:
                    k, v = ("e", p.eng), p.ticket
                if v > w.get(k, 0):
                    w[k] = v
            s = seen[op.eng]
            op.waits = []
            for k, v in w.items():
                if s.get(k, 0) >= v:
                    continue
                s[k] = v
                op.waits.append((k, v))

    def emit(self, final_wait_dsems=()):
        nc = self.nc
        with ExitStack() as es:
            esem = {e: es.enter_context(nc.semaphore("s_" + e)) for e in self.CENG}
            dsem = {k: es.enter_context(nc.semaphore("d_%s" % str(k))) for k in self.dtotal}
            block = es.enter_context(nc.Block())

            def run(ename, eng):
                for op in self.ops:
                    if op.eng != ename:
                        continue
                    for (k, v) in op.waits:
                        eng.wait_ge(dsem[k[1]] if k[0] == "d" else esem[k[1]], v)
                    inst = op.fn(eng)
                    if op.dsem is not None:
                        inst.then_inc(dsem[op.dsem], op.dinc)
                    elif op.sig:
                        inst.then_inc(esem[op.eng], 1)
                if ename == "sp":
                    for k in final_wait_dsems:
                        if k in self.dtotal:
                            eng.wait_ge(dsem[k], self.dtotal[k])

            @block.tensor
            def _(eng):
                run("pe", eng)

            @block.scalar
            def _(eng):
                run("act", eng)

            @block.vector
            def _(eng):
                run("dve", eng)

            @block.gpsimd
            def _(eng):
                run("pool", eng)

            @block.sync
            def _(eng):
                run("sp", eng)


class V:
    __slots__ = ("ap", "toks")

    def __init__(self, ap, toks):
        self.ap, self.toks = ap, toks


def _size(dt):
    return 2 if dt == BF16 else 4


class Buf:
    def __init__(self, arena_t, off, shape, dt, space="sb"):
        self.shape, self.dt, self.off = list(shape), dt, off
        n = int(np.prod(shape)) * _size(dt)
        self.nbytes = n
        ap = arena_t[:, off // 4:(off + n + 3) // 4]
        if dt == BF16:
            ap = ap.bitcast(BF16)
        if len(shape) == 2:
            ap = ap.rearrange("p (a b) -> p a b", a=shape[0])
        elif len(shape) == 3:
            ap = ap.rearrange("p (a b c) -> p a b c", a=shape[0], b=shape[1])
        self.apf = ap
        self.space = space
        self.last_use = None
        self.tok_override = None

    def _toks(self, lo, hi):
        return [(self.space, k) for k in range(lo // 512, (hi - 1) // 512 + 1)]

    def v(self, *idx, p=None):
        ps = slice(None) if p is None else slice(p[0], p[1])
        ap = self.apf[(ps,) + tuple(idx)]
        if self.tok_override is not None:
            return V(ap, list(self.tok_override))
        lo, hi = self.off, self.off + self.nbytes
        if len(self.shape) >= 2 and len(idx) >= 1:
            st = self.nbytes // self.shape[0]
            i0 = idx[0]
            if isinstance(i0, int):
                lo, hi = self.off + i0 * st, self.off + (i0 + 1) * st
            elif isinstance(i0, slice) and i0.start is not None and i0.stop is not None:
                lo, hi = self.off + i0.start * st, self.off + i0.stop * st
        return V(ap, self._toks(lo, hi))


class Arena:
    def __init__(self, nc, es, name, nbytes):
        self.t = es.enter_context(nc.sbuf_tensor(name, [128, nbytes // 4], F32))
        self.off = 0
        self.cap = nbytes

    def alloc(self, shape, dt):
        n = int(np.prod(shape)) * _size(dt)
        off = self.off
        self.off += (n + 511) // 512 * 512
        assert self.off <= self.cap, ("SBUF arena overflow", self.off, self.cap)
        return Buf(self.t, off, shape, dt)


def RT(*vs):
    out = []
    for v in vs:
        if v is None or isinstance(v, (int, float)):
            continue
        out.extend(v.toks)
    return out


def _a(x):
    return x.ap if isinstance(x, V) else x


class K:
    def __init__(self, ntq):
        self.NTQ = ntq
        self.nc = bass.Bass("TRN2", target_bir_lowering=False)
        self.P = Prog(self.nc)
        self.psi = 0
        self.wk = 0
        self.outsems = []

    def mm(self, out, pairs, extra_w=()):
        n = len(pairs)
        op = None
        for i, (l, r) in enumerate(pairs):
            def fn(e, l=l, r=r, i=i):
                return e.matmul(out.ap, lhsT=l.ap, rhs=r.ap, start=(i == 0), stop=(i == n - 1))
            op = self.P.add("pe", fn, reads=RT(l, r), writes=RT(out))
        for s_ in extra_w:
            s_.last_use = op
        return op

    def tr(self, out, in_, ident):
        return self.P.add("pe", lambda e: e.transpose(out=out.ap, in_=in_.ap, identity=ident.ap),
                          reads=RT(in_, ident), writes=RT(out))

    def act(self, out, in_, func, scale=None, bias=None, accum=None):
        kw = {}
        if scale is not None:
            kw["scale"] = _a(scale)
        if bias is not None:
            kw["bias"] = _a(bias)
        if accum is not None:
            kw["accum_out"] = accum.ap
        return self.P.add("act", lambda e: e.activation(out=out.ap, in_=in_.ap, func=func, **kw),
                          reads=RT(in_, scale, bias), writes=RT(out, accum))

    def tt(self, eng, out, a, b, op):
        return self.P.add(eng, lambda e: e.tensor_tensor(out=out.ap, in0=a.ap, in1=b.ap, op=op),
                          reads=RT(a, b), writes=RT(out))

    def ts(self, eng, out, a, s1, op0, s2=None, op1=None):
        def fn(e):
            if op1 is None:
                return e.tensor_scalar(out=out.ap, in0=a.ap, scalar1=_a(s1), scalar2=None, op0=op0)
            return e.tensor_scalar(out=out.ap, in0=a.ap, scalar1=_a(s1), scalar2=_a(s2), op0=op0, op1=op1)
        return self.P.add(eng, fn, reads=RT(a, s1, s2), writes=RT(out))

    def stt(self, out, a, s, b, op0, op1):
        return self.P.add("dve", lambda e: e.scalar_tensor_tensor(out=out.ap, in0=a.ap, scalar=_a(s),
                                                                   in1=b.ap, op0=op0, op1=op1),
                          reads=RT(a, s, b), writes=RT(out))

    def cp(self, eng, out, in_):
        if eng == "act":
            return self.act(out, in_, AF.Copy)
        return self.P.add(eng, lambda e: e.tensor_copy(out=out.ap, in_=in_.ap), reads=RT(in_), writes=RT(out))

    def memset(self, eng, out, val):
        return self.P.add(eng, lambda e: e.memset(out.ap, val), writes=RT(out))

    def dma(self, eng, out, in_, dsem, after=None):
        return self.P.add(eng, lambda e: e.dma_start(out=out.ap, in_=in_.ap), reads=RT(in_), writes=RT(out),
                          dsem=dsem, after=after)

    def psum(self, dt=F32):
        i = self.psi % 8
        self.psi += 1
        ap = self.ps[i][:]
        if dt == BF16:
            ap = ap.bitcast(BF16)
        return V(ap, [("ps", i)])

    def dram(self, name, shape, dt=F32, kind=None):
        if kind is None:
            t = self.nc.dram_tensor(name, list(shape), dt)
        else:
            t = self.nc.dram_tensor(name, list(shape), dt, kind=kind)
        return t.ap()

    def wget(self, loads):
        slot = self.wslots[self.wk % len(self.wslots)]
        sem = "ws%d" % (self.wk % len(self.wslots))
        self.wk += 1
        after = slot.last_use if slot.last_use is not None else self.anchor
        for i, (of, src) in enumerate(loads):
            ov = of(slot)
            wt = [slot.tok_override[i]] if len(loads) > 1 else list(slot.tok_override)
            self.dma("sp", V(ov.ap, wt), src, sem + ("_%d" % i), after=after)
        slot.last_use = None
        return slot

    def wblock(self, wname, c0, ncols, K8=8):
        wap, tok = self.wbf[wname]
        src = V(wap.rearrange("(k p) n -> p k n", p=128)[:, :, c0:c0 + ncols], tok)
        return self.wget([(lambda s: V(s.apf[:, 0:K8 * ncols].rearrange("p (k n) -> p k n", k=K8), s.v().toks), src)])

    def build(self):
        nc, P = self.nc, self.P
        NTQ = self.NTQ
        NP = 4 * NTQ
        TP = NP * 512
        TOWN = NTQ * 512
        ext_in = lambda n, s: self.dram(n, s, F32, "ExternalInput")
        ext_out = lambda n, s: self.dram(n, s, F32, "ExternalOutput")
        xpT = ext_in("xpT", [D, TP])
        xsT = ext_in("xsT", [D, DSEQ])
        ppT = ext_in("ppT", [2, 256, TP])
        psT = ext_in("psT", [2, 256, DSEQ])
        st_pool = ext_in("st_pool", [512, 15])
        st_conv = ext_in("st_conv", [512, 2])
        st_sconv = ext_in("st_sconv", [1024, 3])
        st_ssm = ext_in("st_ssm", [128, 512])
        wsrc = {
            "in0": ext_in("w_in_even", [D, 2048]), "out0": ext_in("w_out_even", [D, D]),
            "in1": ext_in("w_in_odd", [D, 2568]), "out1": ext_in("w_out_odd", [D, D]),
        }
        for l in range(2):
            wsrc["ff1_%d" % l] = ext_in("w_ff1_%d" % l, [D, DFF])
            wsrc["ff3_%d" % l] = ext_in("w_ff3_%d" % l, [D, DFF])
            wsrc["ff2_%d" % l] = ext_in("w_ff2_%d" % l, [DFF, D])
            wsrc["ple_%d" % l] = ext_in("w_ple_%d" % l, [256, D])
            wsrc["gate_%d" % l] = ext_in("w_gate_%d" % l, [D, D])
        pool_w = ext_in("pool_w", [4, 128, 128])
        sgu_wT = ext_in("sgu_wT", [4, 128, 128])
        cols_d = ext_in("cols", [128, NCOLS])
        rows_d = ext_in("rows", [128, NROWS])
        consts_d = ext_in("consts", [128, NCONST])
        flg_d = ext_in("flg", [128, NP])
        invt_d = ext_in("invt", [128, 256])
        ypT = ext_out("ypT", [D, TOWN])
        ysT = ext_out("ysT", [D, DSEQ])
        o_pool = [ext_out("o_pool_p", [512, 15]), ext_out("o_pool_s", [512, 15])]
        o_conv = [ext_out("o_conv_p", [512, 2]), ext_out("o_conv_s", [512, 2])]
        o_sconv = [ext_out("o_sconv_p", [1024, 3]), ext_out("o_sconv_s", [1024, 3])]
        o_ssm = [ext_out("o_ssm_p", [128, 512]), ext_out("o_ssm_s", [128, 512])]
        o_sguv = ext_out("o_sguv", [DSEQ, 512])
        self.wbf = {}
        for n, ap in wsrc.items():
            self.wbf[n] = (self.dram("bf_" + n, ap.shape, BF16),
                           [("dram", "bf_" + n, r0) for r0 in range(0, ap.shape[0], 128)])

        with ExitStack() as es:
            self.ps = [es.enter_context(nc.psum_tensor("ps%d" % i, [128, 512], F32)) for i in range(8)]
            A = Arena(nc, es, "arena", 206 * 1024)
            W = 512
            xf = A.alloc([8, W], F32)
            xb = A.alloc([8, W], BF16)
            m = A.alloc([8, W], BF16)
            g = A.alloc([NJ, W], BF16)
            sq = [A.alloc([W], F32) for _ in range(2)]
            tmpa = [A.alloc([W], F32) for _ in range(3)]
            mu = A.alloc([W], F32)
            rstd = A.alloc([W], F32)
            nmr = A.alloc([W], F32)
            msq = A.alloc([W], F32)
            pf = A.alloc([2, W], F32)
            pb = A.alloc([2, W], BF16)
            self.wslots = [A.alloc([8 * 512], BF16) for _ in range(4)]
            for i_, s_ in enumerate(self.wslots):
                s_.tok_override = [("w", i_, 0), ("w", i_, 1)]
            cols = A.alloc([NCOLS], F32)
            rows = A.alloc([NROWS], F32)
            cst = A.alloc([NCONST], F32)
            ident_b = A.alloc([128], BF16)
            poolw_b = A.alloc([4, 128], BF16)
            sguw_f = A.alloc([4, 128], F32)
            sguw_b = A.alloc([4, 128], BF16)
            wdt = A.alloc([8, 8], BF16)
            a_bc = A.alloc([8], F32)
            flg = A.alloc([NP], F32)
            negm4 = A.alloc([512], F32)
            invt = A.alloc([256], F32)
            ain = A.alloc([4, 16 + W], F32)
            chh = A.alloc([4, 16 + W], F32)
            xbc = A.alloc([8, 16 + W], F32)
            hT = A.alloc([512], F32)
            scr0 = A.off
            T = [A.alloc([16 + W], F32) for _ in range(4)]
            dbf = A.alloc([4, W], BF16)
            bg = A.alloc([4, W], F32)
            cg = A.alloc([4, W], F32)
            ctmp = tmpa
            end0 = A.off
            A.off = scr0
            xc = A.alloc([4, W], F32)
            xcb = A.alloc([4, W], BF16)
            sm = A.alloc([64], F32)
            sm2 = A.alloc([128], F32)
            bdec = A.alloc([8], F32)
            scr1 = A.off
            u_tm = A.alloc([4, 512], F32)
            v_tm = A.alloc([4, 512], F32)
            vn_b = A.alloc([4, 512], BF16)
            yc_b = A.alloc([4, 512], BF16)
            end1 = A.off
            A.off = scr1
            zs = A.alloc([512], F32)
            xs_f = A.alloc([512], F32)
            xs_b = A.alloc([512], BF16)
            bm_tm = A.alloc([256], BF16)
            rhsb = A.alloc([8 * QMAX], F32)
            dec = A.alloc([8 * QMAX], F32)
            m1 = dec
            mT = A.alloc([8 * QMAX], BF16)
            t1 = A.alloc([512], F32)
            t3 = A.alloc([512], F32)
            yd_b = A.alloc([512], BF16)
            xw = A.alloc([512], BF16)
            hT_b = A.alloc([512], BF16)
            sq2 = t3
            A.off = max(A.off, end1)
            A.off = max(A.off, end0)
            print('SBUF bytes/partition used:', A.off)

            def col(i):
                return cols.v(slice(i, i + 1))

            def rowv(i, n, p=None):
                return rows.v(slice(i, i + n), p=p)

            identf = cst.v(slice(C_ID, C_ID + 128))
            tri = lambda q: cst.v(slice(C_TRI, C_TRI + q), p=(0, q))
            onesM = cst.v(slice(C_ONESM, C_ONESM + 128))
            ones = lambda q: cst.v(slice(C_ONES, C_ONES + 128), p=(0, q))

            ntok = (A.cap + 511) // 512
            alltok = [("sb", k_) for k_ in range(ntok)] + [t_ for s_ in self.wslots for t_ in s_.tok_override]
            nfl = A.cap // 4
            for z0 in range(0, nfl, 13184):
                z1 = min(nfl, z0 + 13184)
                self.memset("dve", V(A.t[:, z0:z1], alltok), 0.0)
            self.dma("sp", cols.v(), V(cols_d, []), "c_cols")
            self.dma("sp", rows.v(), V(rows_d, []), "c_rows")
            self.dma("sp", cst.v(), V(consts_d, []), "c_cst")
            self.dma("sp", flg.v(), V(flg_d, []), "c_flg")
            self.dma("sp", invt.v(), V(invt_d, []), "c_invt")
            self.dma("sp", V(sguw_f.apf, sguw_f.v().toks), V(sgu_wT.rearrange("g s t -> s g t"), []), "c_sgu")
            self.dma("pool", V(poolw_b.apf, poolw_b.v().toks), V(pool_w.rearrange("g c d -> c g d"), []), "c_poolw")
            self.dma("pool", V(wdt.apf, wdt.v().toks),
                     V(wsrc["in1"].rearrange("(k p) n -> p k n", p=128)[:, :, 2560:2568], []), "c_wdt")
            order = ["in0", "out0", "ff1_0", "ff3_0", "ff2_0", "gate_0", "ple_0",
                     "in1", "out1", "ff1_1", "ff3_1", "ff2_1", "gate_1", "ple_1"]
            for n in order:
                src = wsrc[n]
                dst, tok = self.wbf[n]
                R = src.shape[0]
                for r0 in range(0, R, 128):
                    self.anchor = self.dma("pool", V(dst[r0:r0 + 128, :], [tok[r0 // 128]]),
                                           V(src[r0:r0 + 128, :], []), "cast_" + n)
            self.cp("dve", ident_b.v(), identf)
            self.ts("dve", V(negm4.apf[:, 0:512].rearrange("p (h t) -> p h t", h=4), negm4.v().toks),
                    V(cst.apf[:, C_TRI:C_TRI + 128].unsqueeze(1).broadcast_to([128, 4, 128]), cst.v().toks),
                    -NEG, ALU.mult, NEG, ALU.add)
            for gi in range(4):
                self.tt("dve", sguw_b.v(gi), sguw_f.v(gi), cst.v(slice(C_TRI, C_TRI + 128)), ALU.mult)
            self.act(a_bc.v(), rowv(R_ALOG, 8), AF.Exp)
            self.ts("dve", a_bc.v(), a_bc.v(), -1.0, ALU.mult)

            tiles = [("p", i * 512, 512, i >= 3 * NTQ) for i in range(NP)] + [("s", 0, DSEQ, False)]
            for ti, (kind, t0, Wt, own) in enumerate(tiles):
                samp = kind == "s"
                xT = xsT if samp else xpT
                yT = ysT if samp else ypT
                pT = psT if samp else ppT
                oi = 1 if samp else 0
                kidx = t0 // 512
                bnd = (kidx // NTQ) if (kind == "p" and kidx % NTQ == 0) else None
                last = samp or (t0 + 512 >= TP)
                cs = slice(0, Wt)
                hs = slice(16, 16 + Wt)
                if samp:
                    self.dma("sp", V(ain.apf[:, :, 1:16], ain.v().toks),
                             V(st_pool.rearrange("(g p) r -> p g r", p=128), []), "st_in1")
                    self.dma("sp", V(chh.apf[:, :, 14:16], chh.v().toks),
                             V(st_conv.rearrange("(g p) r -> p g r", p=128), []), "st_in2")
                    self.dma("sp", V(xbc.apf[:, :, 13:16], xbc.v().toks),
                             V(st_sconv.rearrange("(g p) r -> p g r", p=128), []), "st_in3")
                    self.dma("sp", hT.v(), V(st_ssm, []), "st_in4")
                else:
                    fc = flg.v(slice(kidx, kidx + 1))
                    if os.environ.get("MK_NOFLAG") == "1":
                        if kidx == 0:
                            for bufh in (ain, chh, xbc):
                                self.memset("pool", V(bufh.apf[:, :, 0:16], bufh.v().toks), 0.0)
                            self.memset("pool", hT.v(), 0.0)
                    else:
                        for bufh in (ain, chh, xbc):
                            hv = V(bufh.apf[:, :, 0:16], bufh.v().toks)
                            self.ts("pool", hv, hv, fc, ALU.mult)
                        self.ts("pool", hT.v(), hT.v(), fc, ALU.mult)
                self.dma("sp", V(xf.apf[:, :, cs], xf.v().toks),
                         V(xT.rearrange("(c p) t -> p c t", p=128)[:, :, t0:t0 + Wt], []), "ld_x")
                for c in range(8):
                    self.cp("act" if c % 2 else "pool", xb.v(c, cs), xf.v(c, cs))

                for layer in range(2):
                    if layer == 1 and not (samp or own):
                        self.layer1_mix(locals(), skip=tuple(x for x in os.environ.get("MK_SKIP", "sgu").split(",") if x))
                        continue
                    self.dma("sp", V(pf.apf[:, :, cs], pf.v().toks),
                             V(pT[layer].rearrange("(c p) t -> p c t", p=128)[:, :, t0:t0 + Wt], []), "ld_p")
                    for c in range(2):
                        self.cp("pool", pb.v(c, cs), pf.v(c, cs))
                    if layer == 0:
                        self.layer0_mix(locals())
                    else:
                        self.layer1_mix(locals())
                    self.tail(locals(), layer)
                if samp or own:
                    to = t0 if samp else t0 - 3 * TOWN
                    self.dma("sp", V(yT.rearrange("(c p) t -> p c t", p=128)[:, :, to:to + Wt], [("dram", "y")]),
                             V(xf.apf[:, :, cs], xf.v().toks), "st_y")
                if last:
                    self.dma("sp", V(o_pool[oi].rearrange("(g p) r -> p g r", p=128), [("dram", "o1")]),
                             V(ain.apf[:, :, 1:16], ain.v().toks), "st_o1")
                    self.dma("sp", V(o_conv[oi].rearrange("(g p) r -> p g r", p=128), [("dram", "o2")]),
                             V(chh.apf[:, :, 14:16], chh.v().toks), "st_o2")
                    self.dma("sp", V(o_sconv[oi].rearrange("(g p) r -> p g r", p=128), [("dram", "o3")]),
                             V(xbc.apf[:, :, 13:16], xbc.v().toks), "st_o3")
                    self.dma("sp", V(o_ssm[oi], [("dram", "o4")]), hT.v(), "st_o4")
            P.finalize()
            P.emit(final_wait_dsems=["st_y", "st_o1", "st_o2", "st_o3", "st_o4", "st_v"])
        return nc

    def layernorm(self, L, gcol, bcol):
        xf, xb, sq, mu, rstd, nmr, msq, tmpa = (L[k] for k in ("xf", "xb", "sq", "mu", "rstd", "nmr", "msq", "tmpa"))
        cs, col, onesM = L["cs"], L["col"], L["onesM"]
        ps_mu = self.psum()
        ps_e2 = self.psum()
        Wt = L["Wt"]
        pm = V(ps_mu.ap[:, 0:Wt], ps_mu.toks)
        pe = V(ps_e2.ap[:, 0:Wt], ps_e2.toks)
        n = 8

        for c in range(n):
            self.P.add("pe", (lambda e, c=c: e.matmul(pm.ap, lhsT=onesM.ap, rhs=xf.apf[:, c, cs],
                                                      start=(c == 0), stop=(c == n - 1))),
                       reads=RT(onesM, xf.v(c)), writes=RT(pm))
        for c in range(n):
            s = sq[c % 2]
            self.act(s.v(cs), xf.v(c, cs), AF.Square)
            self.P.add("pe", (lambda e, c=c, s=s: e.matmul(pe.ap, lhsT=onesM.ap, rhs=s.apf[:, cs],
                                                          start=(c == 0), stop=(c == n - 1))),
                       reads=RT(onesM, s.v()), writes=RT(pe))
        self.cp("act", mu.v(cs), pm)
        self.tt("pool", msq.v(cs), mu.v(cs), mu.v(cs), ALU.mult)
        self.tt("dve", rstd.v(cs), pe, msq.v(cs), ALU.subtract)
        self.act(rstd.v(cs), rstd.v(cs), AF.Sqrt, bias=EPS)
        self.P.add("dve", lambda e: e.reciprocal(out=rstd.apf[:, cs], in_=rstd.apf[:, cs]),
                   reads=RT(rstd.v()), writes=RT(rstd.v()))
        self.stt(nmr.v(cs), mu.v(cs), -1.0, rstd.v(cs), ALU.mult, ALU.mult)
        for c in range(n):
            t = tmpa[c % 3]
            self.tt("pool", t.v(cs), xf.v(c, cs), rstd.v(cs), ALU.mult)
            self.tt("dve", t.v(cs), t.v(cs), nmr.v(cs), ALU.add)
            self.act(xf.v(c, cs), t.v(cs), AF.Identity, scale=col(gcol + c), bias=col(bcol + c))
            self.ts("pool", xb.v(c, cs), t.v(cs), col(gcol + c), ALU.mult, col(bcol + c), ALU.add)

    def layer0_mix(self, L):
        xb, m, ain, chh, T, dbf, bg, cg, ctmp, poolw_b = (L[k] for k in
            ("xb", "m", "ain", "chh", "T", "dbf", "bg", "cg", "ctmp", "poolw_b"))
        cs, hs, Wt, col, cst = L["cs"], L["hs"], L["Wt"], L["col"], L["cst"]
        E = 16 + Wt
        def pooling_elem(gi):
            lo = 1
            for lv in range(gi + 1):
                sh = 1 << lv
                eng = "pool" if lv % 2 == 0 else "dve"
                if lv == 0:
                    a_ = ain.v(gi, slice(lo, E))
                    b_ = ain.v(gi, slice(lo - sh, E - sh))
                else:
                    a_ = T[lv - 1].v(slice(lo + sh, E))
                    b_ = T[lv - 1].v(slice(lo, E - sh))
                    lo = lo + sh
                self.tt(eng, T[lv].v(slice(lo, E)), a_, b_, ALU.add)
            S = T[gi]
            win = 2 << gi
            self.stt(dbf.v(gi, cs), S.v(hs), 1.0 / win, ain.v(gi, hs), ALU.mult, ALU.subtract)
            if L["bnd"] is not None:
                ic = L["invt"].v(slice(L["bnd"] * 64 + gi * 16, L["bnd"] * 64 + gi * 16 + 16))
                self.tt("dve", ctmp[2].v(slice(0, 16)), S.v(slice(16, 32)), ic, ALU.mult)
                self.tt("dve", dbf.v(gi, slice(0, 16)), ctmp[2].v(slice(0, 16)), ain.v(gi, slice(16, 32)), ALU.subtract)
            self.cp("pool", ain.v(gi, slice(1, 16)), ain.v(gi, slice(Wt + 1, Wt + 16)))

        def conv(j):
            t = ctmp[j % 2]
            self.ts("pool", t.v(cs), chh.v(j, slice(14, 14 + Wt)), col(C_CONVW + 0 * 4 + j), ALU.mult)
            self.stt(t.v(cs), chh.v(j, slice(15, 15 + Wt)), col(C_CONVW + 1 * 4 + j), t.v(cs), ALU.mult, ALU.add)
            self.stt(t.v(cs), chh.v(j, slice(16, 16 + Wt)), col(C_CONVW + 2 * 4 + j), t.v(cs), ALU.mult, ALU.add)
            self.tt("pool", m.v(4 + j, cs), t.v(cs), bg.v(j, cs), ALU.mult)
            self.cp("pool", chh.v(j, slice(14, 16)), chh.v(j, slice(Wt + 14, Wt + 16)))

        early = os.environ.get("MK_NOEARLYPOOL") != "1"
        for q in range(4):
            slot = self.wblock("in0", q * 512, 512)
            for j in range(4):
                ps = self.psum()
                po = V(ps.ap[:, 0:Wt], ps.toks)
                self.mm(po, [(V(slot.apf[:, k * 512 + j * 128:k * 512 + (j + 1) * 128], slot.v().toks),
                              xb.v(k, cs)) for k in range(8)], extra_w=[slot])
                if q == 0:
                    self.cp("act", ain.v(j, hs), po)
                elif q == 1:
                    self.cp("act", bg.v(j, cs), po)
                elif q == 2:
                    self.cp("act", cg.v(j, cs), po)
                else:
                    self.tt("dve", chh.v(j, hs), po, cg.v(j, cs), ALU.mult)
                    if early:
                        conv(j)
            if q == 0 and early:
                for gi in range(4):
                    pooling_elem(gi)
        for gi in range(4):
            if not early:
                pooling_elem(gi)
            ps = self.psum()
            po = V(ps.ap[:, 0:Wt], ps.toks)
            self.mm(po, [(poolw_b.v(gi), dbf.v(gi, cs))])
            self.act(m.v(gi, cs), po, AF.Identity, scale=col(C_PSCALE + gi))
        if not early:
            for j in range(4):
                conv(j)

    def tail(self, L, layer):
        xf, xb, m, g, tmpa, pb = (L[k] for k in ("xf", "xb", "m", "g", "tmpa", "pb"))
        cs, Wt = L["cs"], L["Wt"]
        wn = "out%d" % layer
        for q in range(2):
            slot = self.wblock(wn, q * 512, 512)
            for j in range(4):
                o = q * 4 + j
                ps = self.psum()
                po = V(ps.ap[:, 0:Wt], ps.toks)
                self.mm(po, [(V(slot.apf[:, k * 512 + j * 128:k * 512 + (j + 1) * 128], slot.v().toks),
                              m.v(k, cs)) for k in range(8)], extra_w=[slot])
                self.stt(xf.v(o, cs), xf.v(o, cs), ALPHA, po, ALU.mult, ALU.add)
        self.layernorm(L, C_LN + layer * 32, C_LN + layer * 32 + 8)
        w1, tok1 = self.wbf["ff1_%d" % layer]
        w3, tok3 = self.wbf["ff3_%d" % layer]
        for jb in range(DFF // 256):
            c0 = jb * 256
            s1 = V(w1.rearrange("(k p) n -> p k n", p=128)[:, :, c0:c0 + 256], tok1)
            s3 = V(w3.rearrange("(k p) n -> p k n", p=128)[:, :, c0:c0 + 256], tok3)
            slot = self.wget([
                (lambda s: V(s.apf[:, 0:4096].rearrange("p (k n) -> p k n", k=8)[:, :, 0:256], s.v().toks), s1),
                (lambda s: V(s.apf[:, 0:4096].rearrange("p (k n) -> p k n", k=8)[:, :, 256:512], s.v().toks), s3)])
            for jj in range(2):
                j = jb * 2 + jj
                p1 = self.psum()
                p3 = self.psum()
                po1 = V(p1.ap[:, 0:Wt], p1.toks)
                po3 = V(p3.ap[:, 0:Wt], p3.toks)
                self.mm(po1, [(V(slot.apf[:, k * 512 + jj * 128:k * 512 + (jj + 1) * 128], slot.v().toks),
                               xb.v(k, cs)) for k in range(8)], extra_w=[slot])
                self.mm(po3, [(V(slot.apf[:, k * 512 + 256 + jj * 128:k * 512 + 256 + (jj + 1) * 128], slot.v().toks),
                               xb.v(k, cs)) for k in range(8)], extra_w=[slot])
                t = tmpa[j % 3]
                self.act(t.v(cs), po1, AF.Silu)
                self.tt("dve", g.v(j, cs), po3, t.v(cs), ALU.mult)
        w2, tok2 = self.wbf["ff2_%d" % layer]
        for o in range(8):
            src = V(w2.rearrange("(j p) n -> p j n", p=128)[:, :, o * 128:(o + 1) * 128], tok2)
            slot = self.wget([(lambda s: V(s.apf[:, 0:NJ * 128].rearrange("p (j n) -> p j n", j=NJ), s.v().toks), src)])
            ps = self.psum()
            po = V(ps.ap[:, 0:Wt], ps.toks)
            self.mm(po, [(V(slot.apf[:, j * 128:(j + 1) * 128], slot.v().toks), g.v(j, cs)) for j in range(NJ)],
                    extra_w=[slot])
            self.stt(xf.v(o, cs), xf.v(o, cs), ALPHA, po, ALU.mult, ALU.add)
        self.layernorm(L, C_LN + layer * 32 + 16, C_LN + layer * 32 + 24)
        wp, tokp = self.wbf["ple_%d" % layer]
        for q in range(2):
            slot = self.wblock("gate_%d" % layer, q * 512, 512)
            srcp = V(wp.rearrange("(k p) n -> p k n", p=128)[:, :, q * 512:(q + 1) * 512], tokp)
            slotp = self.wget([(lambda s: V(s.apf[:, 0:1024].rearrange("p (k n) -> p k n", k=2), s.v().toks), srcp)])
            for j in range(4):
                o = q * 4 + j
                pg = self.psum()
                pp = self.psum()
                pog = V(pg.ap[:, 0:Wt], pg.toks)
                pop = V(pp.ap[:, 0:Wt], pp.toks)
                self.mm(pog, [(V(slot.apf[:, k * 512 + j * 128:k * 512 + (j + 1) * 128], slot.v().toks),
                               xb.v(k, cs)) for k in range(8)], extra_w=[slot])
                self.mm(pop, [(V(slotp.apf[:, k * 512 + j * 128:k * 512 + (j + 1) * 128], slotp.v().toks),
                               pb.v(k, cs)) for k in range(2)], extra_w=[slotp])
                t = tmpa[o % 3]
                self.act(t.v(cs), pog, AF.Sigmoid)
                self.tt("dve", t.v(cs), pop, t.v(cs), ALU.mult)
                self.tt("pool", xf.v(o, cs), xf.v(o, cs), t.v(cs), ALU.add)
        for o in range(8):
            self.cp("act" if o % 2 else "pool", xb.v(o, cs), xf.v(o, cs))

    def layer1_mix(self, L, skip=()):
        g = lambda k: L[k]
        xb, m, xbc, hT, u_tm, v_tm, vn_b, yc_b, xc, xcb, cv, sm = (L[k] for k in
            ("xb", "m", "xbc", "hT", "u_tm", "v_tm", "vn_b", "yc_b", "xc", "xcb", "tmpa", "sm"))
        zs, xs_f, xs_b, bm_tm, rhsb, dec, m1, mT, t1, t3, yd_b, xw, hT_b, sq2 = (L[k] for k in
            ("zs", "xs_f", "xs_b", "bm_tm", "rhsb", "dec", "m1", "mT", "t1", "t3", "yd_b", "xw", "hT_b", "sq2"))
        cs, hs, Wt, col, cst, rowv = L["cs"], L["hs"], L["Wt"], L["col"], L["cst"], L["rowv"]
        wdt, a_bc, ident_b, identf, sguw_b = L["wdt"], L["a_bc"], L["ident_b"], L["identf"], L["sguw_b"]
        tri, ones, samp = L["tri"], L["ones"], L["samp"]
        BT = min(128, Wt)
        NB = Wt // BT
        Q = min(QMAX, Wt)
        NCH = Wt // Q
        if "sgu" not in skip:
            for q, dst in ((0, u_tm), (1, v_tm)):
                slot = self.wblock("in1", q * 512, 512)
                for tb in range(NB):
                    ps = self.psum()
                    po = V(ps.ap[0:BT, :], ps.toks)
                    self.mm(po, [(xb.v(k, slice(tb * BT, (tb + 1) * BT)),
                                  V(slot.apf[:, k * 512:(k + 1) * 512], slot.v().toks)) for k in range(8)], extra_w=[slot])
                    self.act(dst.v(tb, p=(0, BT)), po, AF.Gelu_apprx_tanh)
            for tb in range(NB):
                st = sm.v(slice(0, 6), p=(0, BT))
                mv = sm.v(slice(8, 10), p=(0, BT))
                vv = v_tm.v(tb, p=(0, BT))
                self.P.add("dve", (lambda e, st=st, vv=vv: e.bn_stats(out=st.ap, in_=vv.ap)), reads=RT(vv), writes=RT(st))
                self.P.add("dve", (lambda e, st=st, mv=mv: e.bn_aggr(out=mv.ap, in_=st.ap)), reads=RT(st), writes=RT(mv))
                rs = sm.v(slice(10, 11), p=(0, BT))
                self.act(rs, sm.v(slice(9, 10), p=(0, BT)), AF.Sqrt, bias=EPS)
                self.P.add("dve", (lambda e, rs=rs: e.reciprocal(out=rs.ap, in_=rs.ap)), reads=RT(rs), writes=RT(rs))
                self.ts("dve", vv, vv, sm.v(slice(8, 9), p=(0, BT)), ALU.subtract, rs, ALU.mult)
                self.tt("pool", vv, vv, rowv(R_SGUG, 512, p=(0, BT)), ALU.mult)
                self.tt("pool", vv, vv, rowv(R_SGUB, 512, p=(0, BT)), ALU.add)
                self.cp("act", vn_b.v(tb, p=(0, BT)), vv)
                if samp:
                    self.dma("sp", V(L["o_sguv"], [("dram", "ov")]), vv, "st_v")
            for gi in range(4):
                ps = self.psum()
                po = V(ps.ap[0:BT, 0:NB * 128].rearrange("p (b d) -> p b d", b=NB), ps.toks)
                self.mm(po, [(V(sguw_b.apf[0:BT, gi, 0:BT], sguw_b.v().toks),
                              V(vn_b.apf[0:BT, 0:NB, gi * 128:(gi + 1) * 128], vn_b.v().toks))])
                self.stt(V(yc_b.apf[0:BT, 0:NB, gi * 128:(gi + 1) * 128], yc_b.v().toks), po,
                         cols_p(L, C_SGUB + gi, BT),
                         V(u_tm.apf[0:BT, 0:NB, gi * 128:(gi + 1) * 128], u_tm.v().toks), ALU.add, ALU.mult)
            for ci in range(4):
                ps = self.psum(BF16)
                for tb in range(NB):
                    self.tr(V(ps.ap[:, tb * BT:(tb + 1) * BT], ps.toks),
                            V(yc_b.apf[0:BT, tb, ci * 128:(ci + 1) * 128], yc_b.v().toks),
                            V(ident_b.apf[0:BT, 0:BT], ident_b.v().toks))
                self.cp("act", m.v(ci, cs), V(ps.ap[:, 0:Wt], ps.toks))
        def conv1(j):
            t = cv[j % 2]
            if "cm" in skip and j >= 6:
                self.cp("pool", xbc.v(j, slice(13, 16)), xbc.v(j, slice(Wt + 13, Wt + 16)))
                return
            self.ts("pool", t.v(cs), xbc.v(j, slice(13, 13 + Wt)), col(C_SCW + 0 * 8 + j), ALU.mult)
            for kk in (1, 2, 3):
                self.stt(t.v(cs), xbc.v(j, slice(13 + kk, 13 + kk + Wt)), col(C_SCW + kk * 8 + j), t.v(cs),
                         ALU.mult, ALU.add)
            if j < 4:
                self.act(xc.v(j, cs), t.v(cs), AF.Silu, bias=col(C_SCB + j))
            else:
                self.act(xcb.v(j - 4, cs), t.v(cs), AF.Silu, bias=col(C_SCB + j))
            self.cp("pool", xbc.v(j, slice(13, 16)), xbc.v(j, slice(Wt + 13, Wt + 16)))

        early1 = os.environ.get("MK_NOEARLYCONV1") != "1"
        for q in range(2):
            slot = self.wblock("in1", 1536 + q * 512, 512)
            for j in range(4):
                ps = self.psum()
                po = V(ps.ap[:, 0:Wt], ps.toks)
                self.mm(po, [(V(slot.apf[:, k * 512 + j * 128:k * 512 + (j + 1) * 128], slot.v().toks),
                              xb.v(k, cs)) for k in range(8)], extra_w=[slot])
                self.cp("act", xbc.v(q * 4 + j, hs), po)
                if early1:
                    conv1(q * 4 + j)
        if not early1:
            for j in range(8):
                conv1(j)
        ylvl = 9 if "y" in skip else int(os.environ.get("MK_YLVL", "4")) if skip else 0
        noz = ("y" in skip) or ("z" in skip)
        zslot = None if noz else self.wblock("in1", 1024, 512)
        sm2 = L["sm2"]
        NC8 = 8 * NCH
        psd = self.psum()
        for c in range(NCH):
            self.mm(V(psd.ap[0:Q, c * 8:(c + 1) * 8], psd.toks),
                    [(xb.v(k, slice(c * Q, (c + 1) * Q)), wdt.v(k)) for k in range(8)])
        dtv_all = V(sm2.apf[0:Q, 0:NC8], sm2.v().toks)
        dta_all = V(sm2.apf[0:Q, 32:32 + NC8], sm2.v().toks)
        acm_all = V(sm2.apf[0:Q, 64:64 + NC8], sm2.v().toks)
        eac_all = V(sm2.apf[0:Q, 96:96 + NC8], sm2.v().toks)
        v3 = lambda v_: V(v_.ap.rearrange("p (c h) -> p c h", c=NCH), v_.toks)
        self.tt("dve", v3(dtv_all), V(psd.ap[0:Q, 0:NC8].rearrange("p (c h) -> p c h", c=NCH), psd.toks),
                V(rowv(R_DTB, 8, p=(0, Q)).ap.unsqueeze(1).broadcast_to([Q, NCH, 8]), rowv(R_DTB, 8).toks), ALU.add)
        self.act(dtv_all, dtv_all, AF.Exp)
        self.act(dtv_all, dtv_all, AF.Ln, bias=1.0)
        self.tt("dve", v3(dta_all), v3(dtv_all),
                V(a_bc.apf[0:Q, :].unsqueeze(1).broadcast_to([Q, NCH, 8]), a_bc.v().toks), ALU.mult)
        psa_all = self.psum()
        pa_all = V(psa_all.ap[0:Q, 0:NC8], psa_all.toks)
        self.mm(pa_all, [(V(tri(Q).ap[:, 0:Q], tri(Q).toks), dta_all)])
        self.cp("act", acm_all, pa_all)
        self.act(eac_all, pa_all, AF.Exp)
        for c in range(NCH):
            tc_ = slice(c * Q, (c + 1) * Q)
            pq = (0, Q)
            if not noz:
                ps = self.psum()
                po = V(ps.ap[0:Q, :], ps.toks)
                self.mm(po, [(xb.v(k, tc_), V(zslot.apf[:, k * 512:(k + 1) * 512], zslot.v().toks)) for k in range(8)],
                        extra_w=[zslot])
                self.act(zs.v(p=pq), po, AF.Silu)
            ps = self.psum()
            for j in range(4):
                self.tr(V(ps.ap[0:Q, j * 128:(j + 1) * 128], ps.toks), xc.v(j, tc_), identf)
            self.cp("act", xs_f.v(p=pq), V(ps.ap[0:Q, :], ps.toks))
            self.cp("dve", xs_b.v(p=pq), V(ps.ap[0:Q, :], ps.toks))
            psb = self.psum(BF16)
            for j in range(2):
                self.tr(V(psb.ap[0:Q, j * 128:(j + 1) * 128], psb.toks), xcb.v(j, tc_), ident_b.v())
            self.cp("act", bm_tm.v(p=pq), V(psb.ap[0:Q, 0:256], psb.toks))
            dtv = V(sm2.apf[0:Q, c * 8:(c + 1) * 8], sm2.v().toks)
            dta = V(sm2.apf[0:Q, 32 + c * 8:32 + (c + 1) * 8], sm2.v().toks)
            acm = V(sm2.apf[0:Q, 64 + c * 8:64 + (c + 1) * 8], sm2.v().toks)
            eac = V(sm2.apf[0:Q, 96 + c * 8:96 + (c + 1) * 8], sm2.v().toks)
            toe = sm.v(slice(48, 56), p=pq)
            r3 = V(rhsb.apf[0:Q, 0:8 * Q].rearrange("p (h t) -> p h t", h=8), rhsb.v().toks)
            self.tt("dve", r3, V(tri(Q).ap[:, 0:Q].unsqueeze(1).broadcast_to([Q, 8, Q]), tri(Q).toks),
                    V(dta.ap.unsqueeze(2).broadcast_to([Q, 8, Q]), dta.toks), ALU.mult)
            nhb = 1 if 8 * Q <= 512 else 2
            hpb = 8 // nhb
            psAs = []
            for hb in range(nhb):
                psA = self.psum()
                r3h = V(rhsb.apf[0:Q, hb * hpb * Q:(hb + 1) * hpb * Q].rearrange("p (h t) -> p h t", h=hpb), rhsb.v().toks)
                if nhb == 1:
                    negm = V(cst.apf[0:Q, C_NEG:C_NEG + 8 * 64].rearrange("p (h t) -> p h t", h=8)[:, :, 0:Q], cst.v().toks)
                else:
                    negm = V(L["negm4"].apf[0:Q, 0:hpb * Q].rearrange("p (h t) -> p h t", h=hpb), L["negm4"].v().toks)
                self.mm(V(psA.ap[:, 0:hpb * Q].rearrange("p (h t) -> p h t", h=hpb), psA.toks),
                        [(V(ones(Q).ap[:, 0:128], ones(Q).toks), r3h),
                         (V(identf.ap[0:Q, 0:128], identf.toks), negm)])
                psAs.append(psA)
            d3 = V(dec.apf[0:Q, 0:8 * Q].rearrange("p (h t) -> p h t", h=8), dec.v().toks)
            for hb, psA in enumerate(psAs):
                hsl = slice(hb * hpb, (hb + 1) * hpb)
                self.act(V(L["bdec"].apf[:, hsl], L["bdec"].v().toks),
                         V(psA.ap[:, 0:hpb * Q].rearrange("p (h t) -> p h t", h=hpb)[:, :, Q - 1], psA.toks), AF.Exp)
                d3h = V(dec.apf[0:Q, hb * hpb * Q:(hb + 1) * hpb * Q].rearrange("p (h t) -> p h t", h=hpb), dec.v().toks)
                self.tt("dve", d3h, V(psA.ap[0:Q, 0:hpb * Q].rearrange("p (h t) -> p h t", h=hpb), psA.toks),
                        V(acm.ap[:, hsl].unsqueeze(2).broadcast_to([Q, hpb, Q]), acm.toks), ALU.subtract)
            self.act(d3, d3, AF.Exp)
            self.tt("dve", d3, d3, V(dtv.ap.unsqueeze(2).broadcast_to([Q, 8, Q]), dtv.toks), ALU.mult)
            y3 = lambda b: V(b.apf[0:Q, :].rearrange("p (h d) -> p h d", h=8), b.v().toks)
            if ylvl < 5:
                psc = self.psum()
                for gq in range(2):
                    self.mm(V(psc.ap[0:Q, gq * Q:(gq + 1) * Q], psc.toks), [(xcb.v(gq, tc_), xcb.v(2 + gq, tc_))])
                m4 = V(m1.apf[0:Q, 0:8 * Q].rearrange("p (g r t) -> p g r t", g=2, r=4), m1.v().toks)
                mT4 = V(mT.apf[0:Q, 0:8 * Q].rearrange("p (g r t) -> p g r t", g=2, r=4), mT.v().toks)
                cb4 = V(psc.ap[0:Q, 0:2 * Q].rearrange("p (g t) -> p g t", g=2).unsqueeze(2).broadcast_to([Q, 2, 4, Q]),
                        psc.toks)
                self.tt("dve", mT4, m4, cb4, ALU.mult)
                psY = self.psum()
                for h in range(8):
                    self.mm(V(psY.ap[0:Q, h * 64:(h + 1) * 64], psY.toks),
                            [(V(mT.apf[0:Q, h * Q:(h + 1) * Q], mT.v().toks),
                              V(xs_b.apf[0:Q, h * 64:(h + 1) * 64], xs_b.v().toks))])
            if ylvl < 4:
                self.cp("pool", hT_b.v(), hT.v())
                psS = self.psum()
                for gq in range(2):
                    self.mm(V(psS.ap[0:Q, gq * 256:(gq + 1) * 256], psS.toks),
                            [(xcb.v(2 + gq, tc_), V(hT_b.apf[:, gq * 256:(gq + 1) * 256], hT_b.v().toks))])
            if ylvl < 3:
                y3 = lambda b: V(b.apf[0:Q, :].rearrange("p (h d) -> p h d", h=8), b.v().toks)
                self.tt("dve", y3(t1), V(psS.ap[0:Q, :].rearrange("p (h d) -> p h d", h=8), psS.toks),
                        V(eac.ap.unsqueeze(2).broadcast_to([Q, 8, 64]), eac.toks), ALU.mult)
                self.tt("dve", t1.v(p=pq), t1.v(p=pq), V(psY.ap[0:Q, :], psY.toks), ALU.add)
                self.tt("dve", y3(t3), y3(xs_f),
                        V(rowv(R_SSMD, 8, p=pq).ap.unsqueeze(2).broadcast_to([Q, 8, 64]), rowv(R_SSMD, 8).toks), ALU.mult)
                self.tt("pool", t1.v(p=pq), t1.v(p=pq), t3.v(p=pq), ALU.add)
            if ylvl < 2:
                self.tt("pool", t1.v(p=pq), t1.v(p=pq), zs.v(p=pq), ALU.mult)
                ss = sm.v(slice(56, 58), p=pq)
                for gq in range(2):
                    self.act(V(sq2.apf[0:Q, gq * 256:(gq + 1) * 256], sq2.v().toks),
                             V(t1.apf[0:Q, gq * 256:(gq + 1) * 256], t1.v().toks), AF.Square,
                             accum=sm.v(slice(56 + gq, 57 + gq), p=pq))
                self.act(ss, ss, AF.Sqrt, scale=1.0 / 256.0, bias=EPS)
                self.P.add("dve", (lambda e, ss=ss: e.reciprocal(out=ss.ap, in_=ss.ap)), reads=RT(ss), writes=RT(ss))
                for gq in range(2):
                    self.stt(V(yd_b.apf[0:Q, gq * 256:(gq + 1) * 256], yd_b.v().toks),
                             V(t1.apf[0:Q, gq * 256:(gq + 1) * 256], t1.v().toks),
                             sm.v(slice(56 + gq, 57 + gq), p=pq),
                             rowv(R_NORMW + gq * 256, 256, p=pq), ALU.mult, ALU.mult)
            if ylvl < 1:
                pst = self.psum(BF16)
                for ci in range(4):
                    self.tr(V(pst.ap[:, ci * Q:(ci + 1) * Q], pst.toks),
                            V(yd_b.apf[0:Q, ci * 128:(ci + 1) * 128], yd_b.v().toks),
                            V(ident_b.apf[0:Q, 0:Q], ident_b.v().toks))
                self.cp("act", V(m.apf[:, 4:8, tc_], m.v(slice(4, 8)).toks),
                        V(pst.ap[:, 0:4 * Q].rearrange("p (c t) -> p c t", c=4), pst.toks))
            self.cp("dve", toe, V(dec.apf[0:Q, 0:8 * Q].rearrange("p (h t) -> p h t", h=8)[:, :, Q - 1], dec.v().toks))
            self.tt("dve", y3(xw), y3(xs_f), V(toe.ap.unsqueeze(2).broadcast_to([Q, 8, 64]), toe.toks), ALU.mult)
            psH = self.psum()
            for gq in range(2):
                self.mm(V(psH.ap[:, gq * 256:(gq + 1) * 256], psH.toks),
                        [(V(bm_tm.apf[0:Q, gq * 128:(gq + 1) * 128], bm_tm.v().toks),
                          V(xw.apf[0:Q, gq * 256:(gq + 1) * 256], xw.v().toks))])
            h3 = V(hT.apf[:, :].rearrange("p (h d) -> p h d", h=8), hT.v().toks)
            self.tt("dve", h3, h3,
                    V(L["bdec"].apf[:, 0:8].unsqueeze(2).broadcast_to([128, 8, 64]), L["bdec"].v().toks), ALU.mult)
            self.tt("dve", hT.v(), hT.v(), V(psH.ap[:, :], psH.toks), ALU.add)


def cols_p(L, i, n):
    return L["cols"].v(slice(i, i + 1), p=(0, n))


C_PSCALE = 0
C_CONVW = 4
C_LN = 16
C_SCW = 80
C_SCB = 112
C_SGUB = 120
C_EPS = 124
C_ONE = 125
NCOLS = 128
R_DTB = 0
R_ALOG = 8
R_SSMD = 16
R_SGUG = 32
R_SGUB = 32 + 512
R_NORMW = 32 + 1024
NROWS = 32 + 1536
C_ID = 0
C_TRI = 128
C_ONESM = 256
C_ONES = 384
C_NEG = 512
C_INVC = 1024
NCONST = 1024 + 64


_NC_CACHE = {}


def _pack(inp):
    f = np.float32
    cols = np.zeros((128, NCOLS), f)
    ch = lambda v: np.asarray(v, f).reshape(-1, 128).T
    cols[:, C_PSCALE:C_PSCALE + 4] = ch(inp["pool_scale"][0])
    for k in range(3):
        cols[:, C_CONVW + k * 4:C_CONVW + k * 4 + 4] = ch(inp["conv_w"][0, k])
    for l in range(2):
        b = C_LN + l * 32
        cols[:, b:b + 8] = ch(inp["ln1_g"][l])
        cols[:, b + 8:b + 16] = ch(inp["ln1_b"][l])
        cols[:, b + 16:b + 24] = ch(inp["ln2_g"][l])
        cols[:, b + 24:b + 32] = ch(inp["ln2_b"][l])
    for k in range(4):
        cols[:, C_SCW + k * 8:C_SCW + k * 8 + 8] = ch(inp["ssm_conv_w"][0, k])
    cols[:, C_SCB:C_SCB + 8] = ch(inp["ssm_conv_b"][0])
    cols[:, C_SGUB:C_SGUB + 4] = np.asarray(inp["sgu_b"][0], f).T
    rows = np.zeros((128, NROWS), f)
    bc = lambda v: np.broadcast_to(np.asarray(v, f).reshape(1, -1), (128, np.asarray(v).size))
    rows[:, R_DTB:R_DTB + 8] = bc(inp["ssm_dt_bias"][0])
    rows[:, R_ALOG:R_ALOG + 8] = bc(inp["ssm_a_log"][0])
    rows[:, R_SSMD:R_SSMD + 8] = bc(inp["ssm_d"][0])
    rows[:, R_SGUG:R_SGUG + 512] = bc(inp["sgu_ln_g"][0])
    rows[:, R_SGUB:R_SGUB + 512] = bc(inp["sgu_ln_b"][0])
    rows[:, R_NORMW:R_NORMW + 512] = bc(inp["ssm_norm_w"][0])
    cst = np.zeros((128, NCONST), f)
    cst[:, C_ID:C_ID + 128] = np.eye(128, dtype=f)
    s_i = np.arange(128)[:, None]
    t_i = np.arange(128)[None, :]
    cst[:, C_TRI:C_TRI + 128] = (s_i <= t_i).astype(f)
    cst[:, C_ONESM:C_ONESM + 128] = 1.0 / D
    cst[:, C_ONES:C_ONES + 128] = 1.0
    nm = np.where(s_i[:, :] > np.arange(64)[None, :], NEG, 0.0).astype(f)
    cst[:, C_NEG:C_NEG + 512] = np.tile(nm, (1, 8))
    for gi in range(4):
        win = 2 << gi
        cst[:, C_INVC + gi * 16:C_INVC + gi * 16 + 16] = 1.0 / np.minimum(win, np.arange(16) + 1.0)
    return cols, rows, cst


def kernel(**inp):
    NTQ = int(os.environ.get("MK_NTQ", "4"))
    f = np.float32
    A = lambda a: np.ascontiguousarray(np.asarray(a, f))
    if NTQ not in _NC_CACHE:
        _NC_CACHE[NTQ] = K(NTQ).build()
    nc = _NC_CACHE[NTQ]
    LQ = NTQ * 512
    TP = 4 * LQ
    cols, rows, cst = _pack(inp)
    shared = {
        "w_in_even": A(inp["w_in_even"][0]), "w_out_even": A(inp["w_out_even"][0]),
        "w_in_odd": A(inp["w_in_odd"][0]), "w_out_odd": A(inp["w_out_odd"][0]),
        "pool_w": A(inp["pool_mix_w"][0]),
        "sgu_wT": A(np.transpose(np.asarray(inp["sgu_w"][0]), (0, 2, 1))),
        "cols": cols, "rows": rows, "consts": cst,
    }
    for l in range(2):
        shared["w_ff1_%d" % l] = A(inp["w_ff1"][l])
        shared["w_ff3_%d" % l] = A(inp["w_ff3"][l])
        shared["w_ff2_%d" % l] = A(inp["w_ff2"][l])
        shared["w_ple_%d" % l] = A(inp["w_ple"][l])
        shared["w_gate_%d" % l] = A(inp["w_ple_gate"][l])
    xp = np.asarray(inp["x_prompt"], f)
    pp = np.asarray(inp["p_prompt"], f)
    xs = np.asarray(inp["x_sample"], f)
    psm = np.asarray(inp["p_sample"], f)
    gen = np.zeros((128, 64), f)
    spec = np.zeros((128, 64), f)
    for gi in range(4):
        win = 2 << gi
        gen[:, gi * 16:gi * 16 + 16] = 1.0 / win
        spec[:, gi * 16:gi * 16 + 16] = 1.0 / np.minimum(win, np.arange(16) + 1.0)
    in_maps = []
    for c in range(8):
        b, q = c // 4, c % 4
        d = dict(shared)
        nreal = (q + 1) * LQ
        xT = np.zeros((D, TP), f)
        pT = np.zeros((2, 256, TP), f)
        xT[:, TP - nreal:] = xp[b, :nreal].T
        pT[:, :, TP - nreal:] = np.transpose(pp[:, b, :nreal], (0, 2, 1))
        d["xpT"], d["ppT"] = xT, pT
        k0 = (3 - q) * NTQ
        flg = np.zeros((128, 4 * NTQ), f)
        flg[:, k0 + 1:] = 1.0
        d["flg"] = flg
        invt = np.zeros((128, 256), f)
        for j in range(4):
            invt[:, j * 64:(j + 1) * 64] = spec if j * NTQ == k0 else gen
        d["invt"] = invt
        d["xsT"] = A(xs[c].T)
        d["psT"] = A(np.transpose(psm[:, c], (0, 2, 1)))
        d["st_pool"] = A(np.asarray(inp["state_pool"])[0, c].T)
        d["st_conv"] = A(np.asarray(inp["state_conv"])[0, c].T)
        d["st_sconv"] = A(np.asarray(inp["state_ssm_conv"])[0, c].T)
        d["st_ssm"] = A(np.transpose(np.asarray(inp["state_ssm"])[0, c], (2, 0, 1)).reshape(128, 512))
        in_maps.append(d)
    res = run_bass_kernel_spmd(nc, in_maps, core_ids=list(range(8))).results
    unh = lambda a: np.ascontiguousarray(np.transpose(np.asarray(a, f).reshape(128, 8, 64), (1, 2, 0)))
    y_prompt = np.zeros((2, SEQ, D), f)
    for c in range(8):
        b, q = c // 4, c % 4
        y_prompt[b, q * LQ:(q + 1) * LQ] = res[c]["ypT"].T
    fin = [3, 7]
    y_sample = np.stack([res[c]["ysT"].T for c in range(8)]).astype(f)
    pool_p = np.stack([res[c]["o_pool_p"].T for c in fin])[None].astype(f)
    pool_s = np.stack([res[c]["o_pool_s"].T for c in range(8)])[None].astype(f)
    conv_p = np.stack([res[c]["o_conv_p"].T for c in fin])[None].astype(f)
    conv_s = np.stack([res[c]["o_conv_s"].T for c in range(8)])[None].astype(f)
    sguv = np.stack([res[c]["o_sguv"] for c in range(8)])[None].astype(f)
    sconv_p = np.stack([res[c]["o_sconv_p"].T for c in fin])[None].astype(f)
    sconv_s = np.stack([res[c]["o_sconv_s"].T for c in range(8)])[None].astype(f)
    ssm_p = np.stack([unh(res[c]["o_ssm_p"]) for c in fin])[None].astype(f)
    ssm_s = np.stack([unh(res[c]["o_ssm_s"]) for c in range(8)])[None].astype(f)
    return (np.ascontiguousarray(y_prompt), np.ascontiguousarray(y_sample), pool_p, pool_s, conv_p, conv_s,
            sguv, sconv_p, sconv_s, ssm_p, ssm_s)
```

```python
from contextlib import ExitStack
import os
import numpy as np
import concourse.bass as bass
import concourse.mybir as mybir
from concourse.bass_utils import run_bass_kernel_spmd

F32 = mybir.dt.float32
BF16 = mybir.dt.bfloat16
AF = mybir.ActivationFunctionType
ALU = mybir.AluOpType

D = 1024
SEQ = 8192
DSEQ = 32
DFF = 2816
NJ = DFF // 128
ALPHA = 4.0 ** 0.25
EPS = 1e-5
NEG = -30000.0
QMAX = int(os.environ.get('MK_Q', '128'))
STRICT = os.environ.get('MK_STRICT') == '1'


class Op:
    __slots__ = ("eng", "fn", "reads", "writes", "key", "dsem", "sig", "ticket",
                 "waits", "deps", "dcount", "dinc")


class Prog:
    CENG = ("pe", "act", "dve", "pool")

    def __init__(self, nc):
        self.nc = nc
        self.ops = []
        self.ctr = 0

    def add(self, eng, fn, reads=(), writes=(), dsem=None, after=None, dinc=16):
        op = Op()
        op.eng, op.fn = eng, fn
        op.reads, op.writes = list(reads), list(writes)
        op.dsem, op.dinc = dsem, dinc
        self.ctr += 1
        if after is None:
            op.key = (self.ctr, 0)
        elif after == "start":
            op.key = (0, self.ctr)
        else:
            op.key = (after.key[0], after.key[1] + self.ctr)
        op.sig, op.ticket, op.waits, op.dcount = False, None, [], None
        self.ops.append(op)
        return op

    def finalize(self):
        ops = sorted(self.ops, key=lambda o: o.key)
        self.ops = ops
        last_w, readers, dcount = {}, {}, {}
        for op in ops:
            deps = {}
            for t in op.reads:
                w = last_w.get(t)
                if w is not None:
                    deps[id(w)] = w
            for t in op.writes:
                w = last_w.get(t)
                if w is not None:
                    deps[id(w)] = w
                for r in readers.get(t, ()):
                    deps[id(r)] = r
            deps.pop(id(op), None)
            need = []
            rset = set(op.reads)
            for p in deps.values():
                if p.dsem is not None:
                    need.append(p)
                elif p.eng == op.eng and op.dsem is None:
                    if op.eng == "pe":
                        continue
                    if STRICT or (rset & set(p.writes)):
                        need.append(p)
                else:
                    need.append(p)
            op.deps = need
            for p in need:
                if p.dsem is None:
                    p.sig = True
            for t in op.reads:
                readers.setdefault(t, []).append(op)
            for t in op.writes:
                last_w[t] = op
                readers[t] = []
            if op.dsem is not None:
                dcount[op.dsem] = dcount.get(op.dsem, 0) + op.dinc
                op.dcount = dcount[op.dsem]
        self.dtotal = dcount
        tick = {e: 0 for e in self.CENG}
        for op in ops:
            if op.dsem is None and op.sig:
                tick[op.eng] += 1
                op.ticket = tick[op.eng]
        seen = {e: {} for e in ("pe", "act", "dve", "pool", "sp")}
        for op in ops:
            w = {}
            for p in op.deps:
                if p.dsem is not None:
                    k, v = ("d", p.dsem), p.dcount
                else:
                    k, v = ("e", p.eng), p.ticket
                if v > w.get(k, 0):
                    w[k] = v
            s = seen[op.eng]
            op.waits = []
            for k, v in w.items():
                if s.get(k, 0) >= v:
                    continue
                s[k] = v
                op.waits.append((k, v))

    def emit(self, final_wait_dsems=()):
        nc = self.nc
        with ExitStack() as es:
            esem = {e: es.enter_context(nc.semaphore("s_" + e)) for e in self.CENG}
            dsem = {k: es.enter_context(nc.semaphore("d_%s" % str(k))) for k in self.dtotal}
            block = es.enter_context(nc.Block())

            def run(ename, eng):
                for op in self.ops:
                    if op.eng != ename:
                        continue
                    for (k, v) in op.waits:
                        eng.wait_ge(dsem[k[1]] if k[0] == "d" else esem[k[1]], v)
                    inst = op.fn(eng)
                    if op.dsem is not None:
                        inst.then_inc(dsem[op.dsem], op.dinc)
                    elif op.sig:
                        inst.then_inc(esem[op.eng], 1)
                if ename == "sp":
                    for k in final_wait_dsems:
                        if k in self.dtotal:
                            eng.wait_ge(dsem[k], self.dtotal[k])

            @block.tensor
            def _(eng):
                run("pe", eng)

            @block.scalar
            def _(eng):
                run("act", eng)

            @block.vector
            def _(eng):
                run("dve", eng)

            @block.gpsimd
            def _(eng):
                run("pool", eng)

            @block.sync
            def _(eng):
                run("sp", eng)


class V:
    __slots__ = ("ap", "toks")

    def __init__(self, ap, toks):
        self.ap, self.toks = ap, toks


def _size(dt):
    return 2 if dt == BF16 else 4


class Buf:
    def __init__(self, arena_t, off, shape, dt, space="sb"):
        self.shape, self.dt, self.off = list(shape), dt, off
        n = int(np.prod(shape)) * _size(dt)
        self.nbytes = n
        ap = arena_t[:, off // 4:(off + n + 3) // 4]
        if dt == BF16:
            ap = ap.bitcast(BF16)
        if len(shape) == 2:
            ap = ap.rearrange("p (a b) -> p a b", a=shape[0])
        elif len(shape) == 3:
            ap = ap.rearrange("p (a b c) -> p a b c", a=shape[0], b=shape[1])
        self.apf = ap
        self.space = space
        self.last_use = None
        self.tok_override = None

    def _toks(self, lo, hi):
        return [(self.space, k) for k in range(lo // 512, (hi - 1) // 512 + 1)]

    def v(self, *idx, p=None):
        ps = slice(None) if p is None else slice(p[0], p[1])
        ap = self.apf[(ps,) + tuple(idx)]
        if self.tok_override is not None:
            return V(ap, list(self.tok_override))
        lo, hi = self.off, self.off + self.nbytes
        if len(self.shape) >= 2 and len(idx) >= 1:
            st = self.nbytes // self.shape[0]
            i0 = idx[0]
            if isinstance(i0, int):
                lo, hi = self.off + i0 * st, self.off + (i0 + 1) * st
            elif isinstance(i0, slice) and i0.start is not None and i0.stop is not None:
                lo, hi = self.off + i0.start * st, self.off + i0.stop * st
        return V(ap, self._toks(lo, hi))


class Arena:
    def __init__(self, nc, es, name, nbytes):
        self.t = es.enter_context(nc.sbuf_tensor(name, [128, nbytes // 4], F32))
        self.off = 0
        self.cap = nbytes

    def alloc(self, shape, dt):
        n = int(np.prod(shape)) * _size(dt)
        off = self.off
        self.off += (n + 511) // 512 * 512
        assert self.off <= self.cap, ("SBUF arena overflow", self.off, self.cap)
        return Buf(self.t, off, shape, dt)


def RT(*vs):
    out = []
    for v in vs:
        if v is None or isinstance(v, (int, float)):
            continue
        out.extend(v.toks)
    return out


def _a(x):
    return x.ap if isinstance(x, V) else x


class K:
    def __init__(self, ntq):
        self.NTQ = ntq
        self.nc = bass.Bass("TRN2", target_bir_lowering=False)
        self.P = Prog(self.nc)
        self.psi = 0
        self.wk = 0
        self.outsems = []

    def mm(self, out, pairs, extra_w=()):
        n = len(pairs)
        op = None
        for i, (l, r) in enumerate(pairs):
            def fn(e, l=l, r=r, i=i):
                return e.matmul(out.ap, lhsT=l.ap, rhs=r.ap, start=(i == 0), stop=(i == n - 1))
            op = self.P.add("pe", fn, reads=RT(l, r), writes=RT(out))
        for s_ in extra_w:
            s_.last_use = op
        return op

    def tr(self, out, in_, ident):
        return self.P.add("pe", lambda e: e.transpose(out=out.ap, in_=in_.ap, identity=ident.ap),
                          reads=RT(in_, ident), writes=RT(out))

    def act(self, out, in_, func, scale=None, bias=None, accum=None):
        kw = {}
        if scale is not None:
            kw["scale"] = _a(scale)
        if bias is not None:
            kw["bias"] = _a(bias)
        if accum is not None:
            kw["accum_out"] = accum.ap
        return self.P.add("act", lambda e: e.activation(out=out.ap, in_=in_.ap, func=func, **kw),
                          reads=RT(in_, scale, bias), writes=RT(out, accum))

    def tt(self, eng, out, a, b, op):
        return self.P.add(eng, lambda e: e.tensor_tensor(out=out.ap, in0=a.ap, in1=b.ap, op=op),
                          reads=RT(a, b), writes=RT(out))

    def ts(self, eng, out, a, s1, op0, s2=None, op1=None):
        def fn(e):
            if op1 is None:
                return e.tensor_scalar(out=out.ap, in0=a.ap, scalar1=_a(s1), scalar2=None, op0=op0)
            return e.tensor_scalar(out=out.ap, in0=a.ap, scalar1=_a(s1), scalar2=_a(s2), op0=op0, op1=op1)
        return self.P.add(eng, fn, reads=RT(a, s1, s2), writes=RT(out))

    def stt(self, out, a, s, b, op0, op1):
        return self.P.add("dve", lambda e: e.scalar_tensor_tensor(out=out.ap, in0=a.ap, scalar=_a(s),
                                                                   in1=b.ap, op0=op0, op1=op1),
                          reads=RT(a, s, b), writes=RT(out))

    def cp(self, eng, out, in_):
        if eng == "act":
            return self.act(out, in_, AF.Copy)
        return self.P.add(eng, lambda e: e.tensor_copy(out=out.ap, in_=in_.ap), reads=RT(in_), writes=RT(out))

    def memset(self, eng, out, val):
        return self.P.add(eng, lambda e: e.memset(out.ap, val), writes=RT(out))

    def dma(self, eng, out, in_, dsem, after=None):
        return self.P.add(eng, lambda e: e.dma_start(out=out.ap, in_=in_.ap), reads=RT(in_), writes=RT(out),
                          dsem=dsem, after=after)

    def psum(self, dt=F32):
        i = self.psi % 8
        self.psi += 1
        ap = self.ps[i][:]
        if dt == BF16:
            ap = ap.bitcast(BF16)
        return V(ap, [("ps", i)])

    def dram(self, name, shape, dt=F32, kind=None):
        if kind is None:
            t = self.nc.dram_tensor(name, list(shape), dt)
        else:
            t = self.nc.dram_tensor(name, list(shape), dt, kind=kind)
        return t.ap()

    def wget(self, loads):
        slot = self.wslots[self.wk % len(self.wslots)]
        sem = "ws%d" % (self.wk % len(self.wslots))
        self.wk += 1
        after = slot.last_use if slot.last_use is not None else self.anchor
        for i, (of, src) in enumerate(loads):
            ov = of(slot)
            wt = [slot.tok_override[i]] if len(loads) > 1 else list(slot.tok_override)
            self.dma("sp", V(ov.ap, wt), src, sem + ("_%d" % i), after=after)
        slot.last_use = None
        return slot

    def wblock(self, wname, c0, ncols, K8=8):
        wap, tok = self.wbf[wname]
        src = V(wap.rearrange("(k p) n -> p k n", p=128)[:, :, c0:c0 + ncols], tok)
        return self.wget([(lambda s: V(s.apf[:, 0:K8 * ncols].rearrange("p (k n) -> p k n", k=K8), s.v().toks), src)])

    def build(self):
        nc, P = self.nc, self.P
        NTQ = self.NTQ
        NP = 4 * NTQ
        TP = NP * 512
        TOWN = NTQ * 512
        ext_in = lambda n, s: self.dram(n, s, F32, "ExternalInput")
        ext_out = lambda n, s: self.dram(n, s, F32, "ExternalOutput")
        xpT = ext_in("xpT", [D, TP])
        xsT = ext_in("xsT", [D, DSEQ])
        ppT = ext_in("ppT", [2, 256, TP])
        psT = ext_in("psT", [2, 256, DSEQ])
        st_pool = ext_in("st_pool", [512, 15])
        st_conv = ext_in("st_conv", [512, 2])
        st_sconv = ext_in("st_sconv", [1024, 3])
        st_ssm = ext_in("st_ssm", [128, 512])
        wsrc = {
            "in0": ext_in("w_in_even", [D, 2048]), "out0": ext_in("w_out_even", [D, D]),
            "in1": ext_in("w_in_odd", [D, 2568]), "out1": ext_in("w_out_odd", [D, D]),
        }
        for l in range(2):
            wsrc["ff1_%d" % l] = ext_in("w_ff1_%d" % l, [D, DFF])
            wsrc["ff3_%d" % l] = ext_in("w_ff3_%d" % l, [D, DFF])
            wsrc["ff2_%d" % l] = ext_in("w_ff2_%d" % l, [DFF, D])
            wsrc["ple_%d" % l] = ext_in("w_ple_%d" % l, [256, D])
            wsrc["gate_%d" % l] = ext_in("w_gate_%d" % l, [D, D])
        pool_w = ext_in("pool_w", [4, 128, 128])
        sgu_wT = ext_in("sgu_wT", [4, 128, 128])
        cols_d = ext_in("cols", [128, NCOLS])
        rows_d = ext_in("rows", [128, NROWS])
        consts_d = ext_in("consts", [128, NCONST])
        flg_d = ext_in("flg", [128, NP])
        invt_d = ext_in("invt", [128, 256])
        ypT = ext_out("ypT", [D, TOWN])
        ysT = ext_out("ysT", [D, DSEQ])
        o_pool = [ext_out("o_pool_p", [512, 15]), ext_out("o_pool_s", [512, 15])]
        o_conv = [ext_out("o_conv_p", [512, 2]), ext_out("o_conv_s", [512, 2])]
        o_sconv = [ext_out("o_sconv_p", [1024, 3]), ext_out("o_sconv_s", [1024, 3])]
        o_ssm = [ext_out("o_ssm_p", [128, 512]), ext_out("o_ssm_s", [128, 512])]
        o_sguv = ext_out("o_sguv", [DSEQ, 512])
        self.wbf = {}
        for n, ap in wsrc.items():
            self.wbf[n] = (self.dram("bf_" + n, ap.shape, BF16),
                           [("dram", "bf_" + n, r0) for r0 in range(0, ap.shape[0], 128)])

        with ExitStack() as es:
            self.ps = [es.enter_context(nc.psum_tensor("ps%d" % i, [128, 512], F32)) for i in range(8)]
            A = Arena(nc, es, "arena", 206 * 1024)
            W = 512
            xf = A.alloc([8, W], F32)
            xb = A.alloc([8, W], BF16)
            m = A.alloc([8, W], BF16)
            g = A.alloc([NJ, W], BF16)
            sq = [A.alloc([W], F32) for _ in range(2)]
            tmpa = [A.alloc([W], F32) for _ in range(3)]
            mu = A.alloc([W], F32)
            rstd = A.alloc([W], F32)
            nmr = A.alloc([W], F32)
            msq = A.alloc([W], F32)
            pf = A.alloc([2, W], F32)
            pb = A.alloc([2, W], BF16)
            self.wslots = [A.alloc([8 * 512], BF16) for _ in range(4)]
            for i_, s_ in enumerate(self.wslots):
                s_.tok_override = [("w", i_, 0), ("w", i_, 1)]
            cols = A.alloc([NCOLS], F32)
            rows = A.alloc([NROWS], F32)
            cst = A.alloc([NCONST], F32)
            ident_b = A.alloc([128], BF16)
            poolw_b = A.alloc([4, 128], BF16)
            sguw_f = A.alloc([4, 128], F32)
            sguw_b = A.alloc([4, 128], BF16)
            wdt = A.alloc([8, 8], BF16)
            a_bc = A.alloc([8], F32)
            flg = A.alloc([NP], F32)
            negm4 = A.alloc([512], F32)
            invt = A.alloc([256], F32)
            ain = A.alloc([4, 16 + W], F32)
            chh = A.alloc([4, 16 + W], F32)
            xbc = A.alloc([8, 16 + W], F32)
            hT = A.alloc([512], F32)
            scr0 = A.off
            T = [A.alloc([16 + W], F32) for _ in range(4)]
            dbf = A.alloc([4, W], BF16)
            bg = A.alloc([4, W], F32)
            cg = A.alloc([4, W], F32)
            ctmp = tmpa
            end0 = A.off
            A.off = scr0
            xc = A.alloc([4, W], F32)
            xcb = A.alloc([4, W], BF16)
            sm = A.alloc([64], F32)
            sm2 = A.alloc([128], F32)
            bdec = A.alloc([8], F32)
            scr1 = A.off
            u_tm = A.alloc([4, 512], F32)
            v_tm = A.alloc([4, 512], F32)
            vn_b = A.alloc([4, 512], BF16)
            yc_b = A.alloc([4, 512], BF16)
            end1 = A.off
            A.off = scr1
            zs = A.alloc([512], F32)
            xs_f = A.alloc([512], F32)
            xs_b = A.alloc([512], BF16)
            bm_tm = A.alloc([256], BF16)
            rhsb = A.alloc([8 * QMAX], F32)
            dec = A.alloc([8 * QMAX], F32)
            m1 = dec
            mT = A.alloc([8 * QMAX], BF16)
            t1 = A.alloc([512], F32)
            t3 = A.alloc([512], F32)
            yd_b = A.alloc([512], BF16)
            xw = A.alloc([512], BF16)
            hT_b = A.alloc([512], BF16)
            sq2 = t3
            A.off = max(A.off, end1)
            A.off = max(A.off, end0)
            print('SBUF bytes/partition used:', A.off)

            def col(i):
                return cols.v(slice(i, i + 1))

            def rowv(i, n, p=None):
                return rows.v(slice(i, i + n), p=p)

            identf = cst.v(slice(C_ID, C_ID + 128))
            tri = lambda q: cst.v(slice(C_TRI, C_TRI + q), p=(0, q))
            onesM = cst.v(slice(C_ONESM, C_ONESM + 128))
            ones = lambda q: cst.v(slice(C_ONES, C_ONES + 128), p=(0, q))

            ntok = (A.cap + 511) // 512
            alltok = [("sb", k_) for k_ in range(ntok)] + [t_ for s_ in self.wslots for t_ in s_.tok_override]
            nfl = A.cap // 4
            for z0 in range(0, nfl, 13184):
                z1 = min(nfl, z0 + 13184)
                self.memset("dve", V(A.t[:, z0:z1], alltok), 0.0)
            self.dma("sp", cols.v(), V(cols_d, []), "c_cols")
            self.dma("sp", rows.v(), V(rows_d, []), "c_rows")
            self.dma("sp", cst.v(), V(consts_d, []), "c_cst")
            self.dma("sp", flg.v(), V(flg_d, []), "c_flg")
            self.dma("sp", invt.v(), V(invt_d, []), "c_invt")
            self.dma("sp", V(sguw_f.apf, sguw_f.v().toks), V(sgu_wT.rearrange("g s t -> s g t"), []), "c_sgu")
            self.dma("pool", V(poolw_b.apf, poolw_b.v().toks), V(pool_w.rearrange("g c d -> c g d"), []), "c_poolw")
            self.dma("pool", V(wdt.apf, wdt.v().toks),
                     V(wsrc["in1"].rearrange("(k p) n -> p k n", p=128)[:, :, 2560:2568], []), "c_wdt")
            order = ["in0", "out0", "ff1_0", "ff3_0", "ff2_0", "gate_0", "ple_0",
                     "in1", "out1", "ff1_1", "ff3_1", "ff2_1", "gate_1", "ple_1"]
            for n in order:
                src = wsrc[n]
                dst, tok = self.wbf[n]
                R = src.shape[0]
                for r0 in range(0, R, 128):
                    self.anchor = self.dma("pool", V(dst[r0:r0 + 128, :], [tok[r0 // 128]]),
                                           V(src[r0:r0 + 128, :], []), "cast_" + n)
            self.cp("dve", ident_b.v(), identf)
            self.ts("dve", V(negm4.apf[:, 0:512].rearrange("p (h t) -> p h t", h=4), negm4.v().toks),
                    V(cst.apf[:, C_TRI:C_TRI + 128].unsqueeze(1).broadcast_to([128, 4, 128]), cst.v().toks),
                    -NEG, ALU.mult, NEG, ALU.add)
            for gi in range(4):
                self.tt("dve", sguw_b.v(gi), sguw_f.v(gi), cst.v(slice(C_TRI, C_TRI + 128)), ALU.mult)
            self.act(a_bc.v(), rowv(R_ALOG, 8), AF.Exp)
            self.ts("dve", a_bc.v(), a_bc.v(), -1.0, ALU.mult)

            tiles = [("p", i * 512, 512, i >= 3 * NTQ) for i in range(NP)] + [("s", 0, DSEQ, False)]
            for ti, (kind, t0, Wt, own) in enumerate(tiles):
                samp = kind == "s"
                xT = xsT if samp else xpT
                yT = ysT if samp else ypT
                pT = psT if samp else ppT
                oi = 1 if samp else 0
                kidx = t0 // 512
                bnd = (kidx // NTQ) if (kind == "p" and kidx % NTQ == 0) else None
                last = samp or (t0 + 512 >= TP)
                cs = slice(0, Wt)
                hs = slice(16, 16 + Wt)
                if samp:
                    self.dma("sp", V(ain.apf[:, :, 1:16], ain.v().toks),
                             V(st_pool.rearrange("(g p) r -> p g r", p=128), []), "st_in1")
                    self.dma("sp", V(chh.apf[:, :, 14:16], chh.v().toks),
                             V(st_conv.rearrange("(g p) r -> p g r", p=128), []), "st_in2")
                    self.dma("sp", V(xbc.apf[:, :, 13:16], xbc.v().toks),
                             V(st_sconv.rearrange("(g p) r -> p g r", p=128), []), "st_in3")
                    self.dma("sp", hT.v(), V(st_ssm, []), "st_in4")
                else:
                    fc = flg.v(slice(kidx, kidx + 1))
                    if os.environ.get("MK_NOFLAG") == "1":
                        if kidx == 0:
                            for bufh in (ain, chh, xbc):
                                self.memset("pool", V(bufh.apf[:, :, 0:16], bufh.v().toks), 0.0)
                            self.memset("pool", hT.v(), 0.0)
                    else:
                        for bufh in (ain, chh, xbc):
                            hv = V(bufh.apf[:, :, 0:16], bufh.v().toks)
                            self.ts("pool", hv, hv, fc, ALU.mult)
                        self.ts("pool", hT.v(), hT.v(), fc, ALU.mult)
                self.dma("sp", V(xf.apf[:, :, cs], xf.v().toks),
                         V(xT.rearrange("(c p) t -> p c t", p=128)[:, :, t0:t0 + Wt], []), "ld_x")
                for c in range(8):
                    self.cp("act" if c % 2 else "pool", xb.v(c, cs), xf.v(c, cs))

                for layer in range(2):
                    if layer == 1 and not (samp or own):
                        self.layer1_mix(locals(), skip=tuple(x for x in os.environ.get("MK_SKIP", "sgu").split(",") if x))
                        continue
                    self.dma("sp", V(pf.apf[:, :, cs], pf.v().toks),
                             V(pT[layer].rearrange("(c p) t -> p c t", p=128)[:, :, t0:t0 + Wt], []), "ld_p")
                    for c in range(2):
                        self.cp("pool", pb.v(c, cs), pf.v(c, cs))
                    if layer == 0:
                        self.layer0_mix(locals())
                    else:
                        self.layer1_mix(locals())
                    self.tail(locals(), layer)
                if samp or own:
                    to = t0 if samp else t0 - 3 * TOWN
                    self.dma("sp", V(yT.rearrange("(c p) t -> p c t", p=128)[:, :, to:to + Wt], [("dram", "y")]),
                             V(xf.apf[:, :, cs], xf.v().toks), "st_y")
                if last:
                    self.dma("sp", V(o_pool[oi].rearrange("(g p) r -> p g r", p=128), [("dram", "o1")]),
                             V(ain.apf[:, :, 1:16], ain.v().toks), "st_o1")
                    self.dma("sp", V(o_conv[oi].rearrange("(g p) r -> p g r", p=128), [("dram", "o2")]),
                             V(chh.apf[:, :, 14:16], chh.v().toks), "st_o2")
                    self.dma("sp", V(o_sconv[oi].rearrange("(g p) r -> p g r", p=128), [("dram", "o3")]),
                             V(xbc.apf[:, :, 13:16], xbc.v().toks), "st_o3")
                    self.dma("sp", V(o_ssm[oi], [("dram", "o4")]), hT.v(), "st_o4")
            P.finalize()
            P.emit(final_wait_dsems=["st_y", "st_o1", "st_o2", "st_o3", "st_o4", "st_v"])
        return nc

    def layernorm(self, L, gcol, bcol):
        xf, xb, sq, mu, rstd, nmr, msq, tmpa = (L[k] for k in ("xf", "xb", "sq", "mu", "rstd", "nmr", "msq", "tmpa"))
        cs, col, onesM = L["cs"], L["col"], L["onesM"]
        ps_mu = self.psum()
        ps_e2 = self.psum()
        Wt = L["Wt"]
        pm = V(ps_mu.ap[:, 0:Wt], ps_mu.toks)
        pe = V(ps_e2.ap[:, 0:Wt], ps_e2.toks)
        n = 8

        for c in range(n):
            self.P.add("pe", (lambda e, c=c: e.matmul(pm.ap, lhsT=onesM.ap, rhs=xf.apf[:, c, cs],
                                                      start=(c == 0), stop=(c == n - 1))),
                       reads=RT(onesM, xf.v(c)), writes=RT(pm))
        for c in range(n):
            s = sq[c % 2]
            self.act(s.v(cs), xf.v(c, cs), AF.Square)
            self.P.add("pe", (lambda e, c=c, s=s: e.matmul(pe.ap, lhsT=onesM.ap, rhs=s.apf[:, cs],
                                                          start=(c == 0), stop=(c == n - 1))),
                       reads=RT(onesM, s.v()), writes=RT(pe))
        self.cp("act", mu.v(cs), pm)
        self.tt("pool", msq.v(cs), mu.v(cs), mu.v(cs), ALU.mult)
        self.tt("dve", rstd.v(cs), pe, msq.v(cs), ALU.subtract)
        self.act(rstd.v(cs), rstd.v(cs), AF.Sqrt, bias=EPS)
        self.P.add("dve", lambda e: e.reciprocal(out=rstd.apf[:, cs], in_=rstd.apf[:, cs]),
                   reads=RT(rstd.v()), writes=RT(rstd.v()))
        self.stt(nmr.v(cs), mu.v(cs), -1.0, rstd.v(cs), ALU.mult, ALU.mult)
        for c in range(n):
            t = tmpa[c % 3]
            self.tt("pool", t.v(cs), xf.v(c, cs), rstd.v(cs), ALU.mult)
            self.tt("dve", t.v(cs), t.v(cs), nmr.v(cs), ALU.add)
            self.act(xf.v(c, cs), t.v(cs), AF.Identity, scale=col(gcol + c), bias=col(bcol + c))
            self.act(xb.v(c, cs), t.v(cs), AF.Identity, scale=col(gcol + c), bias=col(bcol + c))

    def layer0_mix(self, L):
        xb, m, ain, chh, T, dbf, bg, cg, ctmp, poolw_b = (L[k] for k in
            ("xb", "m", "ain", "chh", "T", "dbf", "bg", "cg", "ctmp", "poolw_b"))
        cs, hs, Wt, col, cst = L["cs"], L["hs"], L["Wt"], L["col"], L["cst"]
        E = 16 + Wt
        def pooling_elem(gi):
            lo = 1
            for lv in range(gi + 1):
                sh = 1 << lv
                eng = "pool" if lv % 2 == 0 else "dve"
                if lv == 0:
                    a_ = ain.v(gi, slice(lo, E))
                    b_ = ain.v(gi, slice(lo - sh, E - sh))
                else:
                    a_ = T[lv - 1].v(slice(lo + sh, E))
                    b_ = T[lv - 1].v(slice(lo, E - sh))
                    lo = lo + sh
                self.tt(eng, T[lv].v(slice(lo, E)), a_, b_, ALU.add)
            S = T[gi]
            win = 2 << gi
            self.stt(dbf.v(gi, cs), S.v(hs), 1.0 / win, ain.v(gi, hs), ALU.mult, ALU.subtract)
            if L["bnd"] is not None:
                ic = L["invt"].v(slice(L["bnd"] * 64 + gi * 16, L["bnd"] * 64 + gi * 16 + 16))
                self.tt("dve", ctmp[2].v(slice(0, 16)), S.v(slice(16, 32)), ic, ALU.mult)
                self.tt("dve", dbf.v(gi, slice(0, 16)), ctmp[2].v(slice(0, 16)), ain.v(gi, slice(16, 32)), ALU.subtract)
            self.cp("pool", ain.v(gi, slice(1, 16)), ain.v(gi, slice(Wt + 1, Wt + 16)))

        def conv(j):
            t = ctmp[j % 2]
            self.ts("pool", t.v(cs), chh.v(j, slice(14, 14 + Wt)), col(C_CONVW + 0 * 4 + j), ALU.mult)
            self.stt(t.v(cs), chh.v(j, slice(15, 15 + Wt)), col(C_CONVW + 1 * 4 + j), t.v(cs), ALU.mult, ALU.add)
            self.stt(t.v(cs), chh.v(j, slice(16, 16 + Wt)), col(C_CONVW + 2 * 4 + j), t.v(cs), ALU.mult, ALU.add)
            self.tt("pool", m.v(4 + j, cs), t.v(cs), bg.v(j, cs), ALU.mult)
            self.cp("pool", chh.v(j, slice(14, 16)), chh.v(j, slice(Wt + 14, Wt + 16)))

        early = os.environ.get("MK_NOEARLYPOOL") != "1"
        for q in range(4):
            slot = self.wblock("in0", q * 512, 512)
            for j in range(4):
                ps = self.psum()
                po = V(ps.ap[:, 0:Wt], ps.toks)
                self.mm(po, [(V(slot.apf[:, k * 512 + j * 128:k * 512 + (j + 1) * 128], slot.v().toks),
                              xb.v(k, cs)) for k in range(8)], extra_w=[slot])
                if q == 0:
                    self.cp("act", ain.v(j, hs), po)
                elif q == 1:
                    self.cp("act", bg.v(j, cs), po)
                elif q == 2:
                    self.cp("act", cg.v(j, cs), po)
                else:
                    self.tt("dve", chh.v(j, hs), po, cg.v(j, cs), ALU.mult)
                    if early:
                        conv(j)
            if q == 0 and early:
                for gi in range(4):
                    pooling_elem(gi)
        for gi in range(4):
            if not early:
                pooling_elem(gi)
            ps = self.psum()
            po = V(ps.ap[:, 0:Wt], ps.toks)
            self.mm(po, [(poolw_b.v(gi), dbf.v(gi, cs))])
            self.act(m.v(gi, cs), po, AF.Identity, scale=col(C_PSCALE + gi))
        if not early:
            for j in range(4):
                conv(j)

    def tail(self, L, layer):
        xf, xb, m, g, tmpa, pb = (L[k] for k in ("xf", "xb", "m", "g", "tmpa", "pb"))
        cs, Wt = L["cs"], L["Wt"]
        wn = "out%d" % layer
        for q in range(2):
            slot = self.wblock(wn, q * 512, 512)
            for j in range(4):
                o = q * 4 + j
                ps = self.psum()
                po = V(ps.ap[:, 0:Wt], ps.toks)
                self.mm(po, [(V(slot.apf[:, k * 512 + j * 128:k * 512 + (j + 1) * 128], slot.v().toks),
                              m.v(k, cs)) for k in range(8)], extra_w=[slot])
                self.stt(xf.v(o, cs), xf.v(o, cs), ALPHA, po, ALU.mult, ALU.add)
        self.layernorm(L, C_LN + layer * 32, C_LN + layer * 32 + 8)
        w1, tok1 = self.wbf["ff1_%d" % layer]
        w3, tok3 = self.wbf["ff3_%d" % layer]
        for jb in range(DFF // 256):
            c0 = jb * 256
            s1 = V(w1.rearrange("(k p) n -> p k n", p=128)[:, :, c0:c0 + 256], tok1)
            s3 = V(w3.rearrange("(k p) n -> p k n", p=128)[:, :, c0:c0 + 256], tok3)
            slot = self.wget([
                (lambda s: V(s.apf[:, 0:4096].rearrange("p (k n) -> p k n", k=8)[:, :, 0:256], s.v().toks), s1),
                (lambda s: V(s.apf[:, 0:4096].rearrange("p (k n) -> p k n", k=8)[:, :, 256:512], s.v().toks), s3)])
            for jj in range(2):
                j = jb * 2 + jj
                p1 = self.psum()
                p3 = self.psum()
                po1 = V(p1.ap[:, 0:Wt], p1.toks)
                po3 = V(p3.ap[:, 0:Wt], p3.toks)
                self.mm(po1, [(V(slot.apf[:, k * 512 + jj * 128:k * 512 + (jj + 1) * 128], slot.v().toks),
                               xb.v(k, cs)) for k in range(8)], extra_w=[slot])
                self.mm(po3, [(V(slot.apf[:, k * 512 + 256 + jj * 128:k * 512 + 256 + (jj + 1) * 128], slot.v().toks),
                               xb.v(k, cs)) for k in range(8)], extra_w=[slot])
                t = tmpa[j % 3]
                self.act(t.v(cs), po1, AF.Silu)
                self.tt("dve", g.v(j, cs), po3, t.v(cs), ALU.mult)
        w2, tok2 = self.wbf["ff2_%d" % layer]
        for o in range(8):
            src = V(w2.rearrange("(j p) n -> p j n", p=128)[:, :, o * 128:(o + 1) * 128], tok2)
            slot = self.wget([(lambda s: V(s.apf[:, 0:NJ * 128].rearrange("p (j n) -> p j n", j=NJ), s.v().toks), src)])
            ps = self.psum()
            po = V(ps.ap[:, 0:Wt], ps.toks)
            self.mm(po, [(V(slot.apf[:, j * 128:(j + 1) * 128], slot.v().toks), g.v(j, cs)) for j in range(NJ)],
                    extra_w=[slot])
            self.stt(xf.v(o, cs), xf.v(o, cs), ALPHA, po, ALU.mult, ALU.add)
        self.layernorm(L, C_LN + layer * 32 + 16, C_LN + layer * 32 + 24)
        wp, tokp = self.wbf["ple_%d" % layer]
        for q in range(2):
            slot = self.wblock("gate_%d" % layer, q * 512, 512)
            srcp = V(wp.rearrange("(k p) n -> p k n", p=128)[:, :, q * 512:(q + 1) * 512], tokp)
            slotp = self.wget([(lambda s: V(s.apf[:, 0:1024].rearrange("p (k n) -> p k n", k=2), s.v().toks), srcp)])
            for j in range(4):
                o = q * 4 + j
                pg = self.psum()
                pp = self.psum()
                pog = V(pg.ap[:, 0:Wt], pg.toks)
                pop = V(pp.ap[:, 0:Wt], pp.toks)
                self.mm(pog, [(V(slot.apf[:, k * 512 + j * 128:k * 512 + (j + 1) * 128], slot.v().toks),
                               xb.v(k, cs)) for k in range(8)], extra_w=[slot])
                self.mm(pop, [(V(slotp.apf[:, k * 512 + j * 128:k * 512 + (j + 1) * 128], slotp.v().toks),
                               pb.v(k, cs)) for k in range(2)], extra_w=[slotp])
                t = tmpa[o % 3]
                self.act(t.v(cs), pog, AF.Sigmoid)
                self.tt("dve", t.v(cs), pop, t.v(cs), ALU.mult)
                self.tt("pool", xf.v(o, cs), xf.v(o, cs), t.v(cs), ALU.add)
        for o in range(8):
            self.cp("act" if o % 2 else "pool", xb.v(o, cs), xf.v(o, cs))

    def layer1_mix(self, L, skip=()):
        g = lambda k: L[k]
        xb, m, xbc, hT, u_tm, v_tm, vn_b, yc_b, xc, xcb, cv, sm = (L[k] for k in
            ("xb", "m", "xbc", "hT", "u_tm", "v_tm", "vn_b", "yc_b", "xc", "xcb", "tmpa", "sm"))
        zs, xs_f, xs_b, bm_tm, rhsb, dec, m1, mT, t1, t3, yd_b, xw, hT_b, sq2 = (L[k] for k in
            ("zs", "xs_f", "xs_b", "bm_tm", "rhsb", "dec", "m1", "mT", "t1", "t3", "yd_b", "xw", "hT_b", "sq2"))
        cs, hs, Wt, col, cst, rowv = L["cs"], L["hs"], L["Wt"], L["col"], L["cst"], L["rowv"]
        wdt, a_bc, ident_b, identf, sguw_b = L["wdt"], L["a_bc"], L["ident_b"], L["identf"], L["sguw_b"]
        tri, ones, samp = L["tri"], L["ones"], L["samp"]
        BT = min(128, Wt)
        NB = Wt // BT
        Q = min(QMAX, Wt)
        NCH = Wt // Q
        if "sgu" not in skip:
            for q, dst in ((0, u_tm), (1, v_tm)):
                slot = self.wblock("in1", q * 512, 512)
                for tb in range(NB):
                    ps = self.psum()
                    po = V(ps.ap[0:BT, :], ps.toks)
                    self.mm(po, [(xb.v(k, slice(tb * BT, (tb + 1) * BT)),
                                  V(slot.apf[:, k * 512:(k + 1) * 512], slot.v().toks)) for k in range(8)], extra_w=[slot])
                    self.act(dst.v(tb, p=(0, BT)), po, AF.Gelu_apprx_tanh)
            for tb in range(NB):
                st = sm.v(slice(0, 6), p=(0, BT))
                mv = sm.v(slice(8, 10), p=(0, BT))
                vv = v_tm.v(tb, p=(0, BT))
                self.P.add("dve", (lambda e, st=st, vv=vv: e.bn_stats(out=st.ap, in_=vv.ap)), reads=RT(vv), writes=RT(st))
                self.P.add("dve", (lambda e, st=st, mv=mv: e.bn_aggr(out=mv.ap, in_=st.ap)), reads=RT(st), writes=RT(mv))
                rs = sm.v(slice(10, 11), p=(0, BT))
                self.act(rs, sm.v(slice(9, 10), p=(0, BT)), AF.Sqrt, bias=EPS)
                self.P.add("dve", (lambda e, rs=rs: e.reciprocal(out=rs.ap, in_=rs.ap)), reads=RT(rs), writes=RT(rs))
                self.ts("dve", vv, vv, sm.v(slice(8, 9), p=(0, BT)), ALU.subtract, rs, ALU.mult)
                self.tt("pool", vv, vv, rowv(R_SGUG, 512, p=(0, BT)), ALU.mult)
                self.tt("pool", vv, vv, rowv(R_SGUB, 512, p=(0, BT)), ALU.add)
                self.cp("act", vn_b.v(tb, p=(0, BT)), vv)
                if samp:
                    self.dma("sp", V(L["o_sguv"], [("dram", "ov")]), vv, "st_v")
            for gi in range(4):
                ps = self.psum()
                po = V(ps.ap[0:BT, 0:NB * 128].rearrange("p (b d) -> p b d", b=NB), ps.toks)
                self.mm(po, [(V(sguw_b.apf[0:BT, gi, 0:BT], sguw_b.v().toks),
                              V(vn_b.apf[0:BT, 0:NB, gi * 128:(gi + 1) * 128], vn_b.v().toks))])
                self.stt(V(yc_b.apf[0:BT, 0:NB, gi * 128:(gi + 1) * 128], yc_b.v().toks), po,
                         cols_p(L, C_SGUB + gi, BT),
                         V(u_tm.apf[0:BT, 0:NB, gi * 128:(gi + 1) * 128], u_tm.v().toks), ALU.add, ALU.mult)
            for ci in range(4):
                ps = self.psum(BF16)
                for tb in range(NB):
                    self.tr(V(ps.ap[:, tb * BT:(tb + 1) * BT], ps.toks),
                            V(yc_b.apf[0:BT, tb, ci * 128:(ci + 1) * 128], yc_b.v().toks),
                            V(ident_b.apf[0:BT, 0:BT], ident_b.v().toks))
                self.cp("act", m.v(ci, cs), V(ps.ap[:, 0:Wt], ps.toks))
        for q in range(2):
            slot = self.wblock("in1", 1536 + q * 512, 512)
            for j in range(4):
                ps = self.psum()
                po = V(ps.ap[:, 0:Wt], ps.toks)
                self.mm(po, [(V(slot.apf[:, k * 512 + j * 128:k * 512 + (j + 1) * 128], slot.v().toks),
                              xb.v(k, cs)) for k in range(8)], extra_w=[slot])
                self.cp("act", xbc.v(q * 4 + j, hs), po)
        for j in range(8):
            t = cv[j % 2]
            if "cm" in skip and j >= 6:
                self.cp("pool", xbc.v(j, slice(13, 16)), xbc.v(j, slice(Wt + 13, Wt + 16)))
                continue
            self.ts("pool", t.v(cs), xbc.v(j, slice(13, 13 + Wt)), col(C_SCW + 0 * 8 + j), ALU.mult)
            for kk in (1, 2, 3):
                self.stt(t.v(cs), xbc.v(j, slice(13 + kk, 13 + kk + Wt)), col(C_SCW + kk * 8 + j), t.v(cs),
                         ALU.mult, ALU.add)
            if j < 4:
                self.act(xc.v(j, cs), t.v(cs), AF.Silu, bias=col(C_SCB + j))
            else:
                self.act(xcb.v(j - 4, cs), t.v(cs), AF.Silu, bias=col(C_SCB + j))
            self.cp("pool", xbc.v(j, slice(13, 16)), xbc.v(j, slice(Wt + 13, Wt + 16)))
        ylvl = 9 if "y" in skip else int(os.environ.get("MK_YLVL", "4")) if skip else 0
        noz = ("y" in skip) or ("z" in skip)
        zslot = None if noz else self.wblock("in1", 1024, 512)
        sm2 = L["sm2"]
        NC8 = 8 * NCH
        psd = self.psum()
        for c in range(NCH):
            self.mm(V(psd.ap[0:Q, c * 8:(c + 1) * 8], psd.toks),
                    [(xb.v(k, slice(c * Q, (c + 1) * Q)), wdt.v(k)) for k in range(8)])
        dtv_all = V(sm2.apf[0:Q, 0:NC8], sm2.v().toks)
        dta_all = V(sm2.apf[0:Q, 32:32 + NC8], sm2.v().toks)
        acm_all = V(sm2.apf[0:Q, 64:64 + NC8], sm2.v().toks)
        eac_all = V(sm2.apf[0:Q, 96:96 + NC8], sm2.v().toks)
        v3 = lambda v_: V(v_.ap.rearrange("p (c h) -> p c h", c=NCH), v_.toks)
        self.tt("dve", v3(dtv_all), V(psd.ap[0:Q, 0:NC8].rearrange("p (c h) -> p c h", c=NCH), psd.toks),
                V(rowv(R_DTB, 8, p=(0, Q)).ap.unsqueeze(1).broadcast_to([Q, NCH, 8]), rowv(R_DTB, 8).toks), ALU.add)
        self.act(dtv_all, dtv_all, AF.Exp)
        self.act(dtv_all, dtv_all, AF.Ln, bias=1.0)
        self.tt("dve", v3(dta_all), v3(dtv_all),
                V(a_bc.apf[0:Q, :].unsqueeze(1).broadcast_to([Q, NCH, 8]), a_bc.v().toks), ALU.mult)
        psa_all = self.psum()
        pa_all = V(psa_all.ap[0:Q, 0:NC8], psa_all.toks)
        self.mm(pa_all, [(V(tri(Q).ap[:, 0:Q], tri(Q).toks), dta_all)])
        self.cp("act", acm_all, pa_all)
        self.act(eac_all, pa_all, AF.Exp)
        for c in range(NCH):
            tc_ = slice(c * Q, (c + 1) * Q)
            pq = (0, Q)
            if not noz:
                ps = self.psum()
                po = V(ps.ap[0:Q, :], ps.toks)
                self.mm(po, [(xb.v(k, tc_), V(zslot.apf[:, k * 512:(k + 1) * 512], zslot.v().toks)) for k in range(8)],
                        extra_w=[zslot])
                self.act(zs.v(p=pq), po, AF.Silu)
            ps = self.psum()
            for j in range(4):
                self.tr(V(ps.ap[0:Q, j * 128:(j + 1) * 128], ps.toks), xc.v(j, tc_), identf)
            self.cp("act", xs_f.v(p=pq), V(ps.ap[0:Q, :], ps.toks))
            self.cp("dve", xs_b.v(p=pq), V(ps.ap[0:Q, :], ps.toks))
            psb = self.psum(BF16)
            for j in range(2):
                self.tr(V(psb.ap[0:Q, j * 128:(j + 1) * 128], psb.toks), xcb.v(j, tc_), ident_b.v())
            self.cp("act", bm_tm.v(p=pq), V(psb.ap[0:Q, 0:256], psb.toks))
            dtv = V(sm2.apf[0:Q, c * 8:(c + 1) * 8], sm2.v().toks)
            dta = V(sm2.apf[0:Q, 32 + c * 8:32 + (c + 1) * 8], sm2.v().toks)
            acm = V(sm2.apf[0:Q, 64 + c * 8:64 + (c + 1) * 8], sm2.v().toks)
            eac = V(sm2.apf[0:Q, 96 + c * 8:96 + (c + 1) * 8], sm2.v().toks)
            toe = sm.v(slice(48, 56), p=pq)
            r3 = V(rhsb.apf[0:Q, 0:8 * Q].rearrange("p (h t) -> p h t", h=8), rhsb.v().toks)
            self.tt("dve", r3, V(tri(Q).ap[:, 0:Q].unsqueeze(1).broadcast_to([Q, 8, Q]), tri(Q).toks),
                    V(dta.ap.unsqueeze(2).broadcast_to([Q, 8, Q]), dta.toks), ALU.mult)
            nhb = 1 if 8 * Q <= 512 else 2
            hpb = 8 // nhb
            psAs = []
            for hb in range(nhb):
                psA = self.psum()
                r3h = V(rhsb.apf[0:Q, hb * hpb * Q:(hb + 1) * hpb * Q].rearrange("p (h t) -> p h t", h=hpb), rhsb.v().toks)
                if nhb == 1:
                    negm = V(cst.apf[0:Q, C_NEG:C_NEG + 8 * 64].rearrange("p (h t) -> p h t", h=8)[:, :, 0:Q], cst.v().toks)
                else:
                    negm = V(L["negm4"].apf[0:Q, 0:hpb * Q].rearrange("p (h t) -> p h t", h=hpb), L["negm4"].v().toks)
                self.mm(V(psA.ap[:, 0:hpb * Q].rearrange("p (h t) -> p h t", h=hpb), psA.toks),
                        [(V(ones(Q).ap[:, 0:128], ones(Q).toks), r3h),
                         (V(identf.ap[0:Q, 0:128], identf.toks), negm)])
                psAs.append(psA)
            d3 = V(dec.apf[0:Q, 0:8 * Q].rearrange("p (h t) -> p h t", h=8), dec.v().toks)
            for hb, psA in enumerate(psAs):
                hsl = slice(hb * hpb, (hb + 1) * hpb)
                self.act(V(L["bdec"].apf[:, hsl], L["bdec"].v().toks),
                         V(psA.ap[:, 0:hpb * Q].rearrange("p (h t) -> p h t", h=hpb)[:, :, Q - 1], psA.toks), AF.Exp)
                d3h = V(dec.apf[0:Q, hb * hpb * Q:(hb + 1) * hpb * Q].rearrange("p (h t) -> p h t", h=hpb), dec.v().toks)
                self.tt("dve", d3h, V(psA.ap[0:Q, 0:hpb * Q].rearrange("p (h t) -> p h t", h=hpb), psA.toks),
                        V(acm.ap[:, hsl].unsqueeze(2).broadcast_to([Q, hpb, Q]), acm.toks), ALU.subtract)
            self.act(d3, d3, AF.Exp)
            self.tt("dve", d3, d3, V(dtv.ap.unsqueeze(2).broadcast_to([Q, 8, Q]), dtv.toks), ALU.mult)
            y3 = lambda b: V(b.apf[0:Q, :].rearrange("p (h d) -> p h d", h=8), b.v().toks)
            if ylvl < 5:
                psc = self.psum()
                for gq in range(2):
                    self.mm(V(psc.ap[0:Q, gq * Q:(gq + 1) * Q], psc.toks), [(xcb.v(gq, tc_), xcb.v(2 + gq, tc_))])
                m4 = V(m1.apf[0:Q, 0:8 * Q].rearrange("p (g r t) -> p g r t", g=2, r=4), m1.v().toks)
                mT4 = V(mT.apf[0:Q, 0:8 * Q].rearrange("p (g r t) -> p g r t", g=2, r=4), mT.v().toks)
                cb4 = V(psc.ap[0:Q, 0:2 * Q].rearrange("p (g t) -> p g t", g=2).unsqueeze(2).broadcast_to([Q, 2, 4, Q]),
                        psc.toks)
                self.tt("dve", mT4, m4, cb4, ALU.mult)
                psY = self.psum()
                for h in range(8):
                    self.mm(V(psY.ap[0:Q, h * 64:(h + 1) * 64], psY.toks),
                            [(V(mT.apf[0:Q, h * Q:(h + 1) * Q], mT.v().toks),
                              V(xs_b.apf[0:Q, h * 64:(h + 1) * 64], xs_b.v().toks))])
            if ylvl < 4:
                self.cp("pool", hT_b.v(), hT.v())
                psS = self.psum()
                for gq in range(2):
                    self.mm(V(psS.ap[0:Q, gq * 256:(gq + 1) * 256], psS.toks),
                            [(xcb.v(2 + gq, tc_), V(hT_b.apf[:, gq * 256:(gq + 1) * 256], hT_b.v().toks))])
            if ylvl < 3:
                y3 = lambda b: V(b.apf[0:Q, :].rearrange("p (h d) -> p h d", h=8), b.v().toks)
                self.tt("dve", y3(t1), V(psS.ap[0:Q, :].rearrange("p (h d) -> p h d", h=8), psS.toks),
                        V(eac.ap.unsqueeze(2).broadcast_to([Q, 8, 64]), eac.toks), ALU.mult)
                self.tt("dve", t1.v(p=pq), t1.v(p=pq), V(psY.ap[0:Q, :], psY.toks), ALU.add)
                self.tt("dve", y3(t3), y3(xs_f),
                        V(rowv(R_SSMD, 8, p=pq).ap.unsqueeze(2).broadcast_to([Q, 8, 64]), rowv(R_SSMD, 8).toks), ALU.mult)
                self.tt("pool", t1.v(p=pq), t1.v(p=pq), t3.v(p=pq), ALU.add)
            if ylvl < 2:
                self.tt("pool", t1.v(p=pq), t1.v(p=pq), zs.v(p=pq), ALU.mult)
                ss = sm.v(slice(56, 58), p=pq)
                for gq in range(2):
                    self.act(V(sq2.apf[0:Q, gq * 256:(gq + 1) * 256], sq2.v().toks),
                             V(t1.apf[0:Q, gq * 256:(gq + 1) * 256], t1.v().toks), AF.Square,
                             accum=sm.v(slice(56 + gq, 57 + gq), p=pq))
                self.act(ss, ss, AF.Sqrt, scale=1.0 / 256.0, bias=EPS)
                self.P.add("dve", (lambda e, ss=ss: e.reciprocal(out=ss.ap, in_=ss.ap)), reads=RT(ss), writes=RT(ss))
                for gq in range(2):
                    self.stt(V(yd_b.apf[0:Q, gq * 256:(gq + 1) * 256], yd_b.v().toks),
                             V(t1.apf[0:Q, gq * 256:(gq + 1) * 256], t1.v().toks),
                             sm.v(slice(56 + gq, 57 + gq), p=pq),
                             rowv(R_NORMW + gq * 256, 256, p=pq), ALU.mult, ALU.mult)
            if ylvl < 1:
                pst = self.psum(BF16)
                for ci in range(4):
                    self.tr(V(pst.ap[:, ci * Q:(ci + 1) * Q], pst.toks),
                            V(yd_b.apf[0:Q, ci * 128:(ci + 1) * 128], yd_b.v().toks),
                            V(ident_b.apf[0:Q, 0:Q], ident_b.v().toks))
                self.cp("act", V(m.apf[:, 4:8, tc_], m.v(slice(4, 8)).toks),
                        V(pst.ap[:, 0:4 * Q].rearrange("p (c t) -> p c t", c=4), pst.toks))
            self.cp("dve", toe, V(dec.apf[0:Q, 0:8 * Q].rearrange("p (h t) -> p h t", h=8)[:, :, Q - 1], dec.v().toks))
            self.tt("dve", y3(xw), y3(xs_f), V(toe.ap.unsqueeze(2).broadcast_to([Q, 8, 64]), toe.toks), ALU.mult)
            psH = self.psum()
            for gq in range(2):
                self.mm(V(psH.ap[:, gq * 256:(gq + 1) * 256], psH.toks),
                        [(V(bm_tm.apf[0:Q, gq * 128:(gq + 1) * 128], bm_tm.v().toks),
                          V(xw.apf[0:Q, gq * 256:(gq + 1) * 256], xw.v().toks))])
            h3 = V(hT.apf[:, :].rearrange("p (h d) -> p h d", h=8), hT.v().toks)
            self.tt("dve", h3, h3,
                    V(L["bdec"].apf[:, 0:8].unsqueeze(2).broadcast_to([128, 8, 64]), L["bdec"].v().toks), ALU.mult)
            self.tt("dve", hT.v(), hT.v(), V(psH.ap[:, :], psH.toks), ALU.add)


def cols_p(L, i, n):
    return L["cols"].v(slice(i, i + 1), p=(0, n))


C_PSCALE = 0
C_CONVW = 4
C_LN = 16
C_SCW = 80
C_SCB = 112
C_SGUB = 120
C_EPS = 124
C_ONE = 125
NCOLS = 128
R_DTB = 0
R_ALOG = 8
R_SSMD = 16
R_SGUG = 32
R_SGUB = 32 + 512
R_NORMW = 32 + 1024
NROWS = 32 + 1536
C_ID = 0
C_TRI = 128
C_ONESM = 256
C_ONES = 384
C_NEG = 512
C_INVC = 1024
NCONST = 1024 + 64


_NC_CACHE = {}


def _pack(inp):
    f = np.float32
    cols = np.zeros((128, NCOLS), f)
    ch = lambda v: np.asarray(v, f).reshape(-1, 128).T
    cols[:, C_PSCALE:C_PSCALE + 4] = ch(inp["pool_scale"][0])
    for k in range(3):
        cols[:, C_CONVW + k * 4:C_CONVW + k * 4 + 4] = ch(inp["conv_w"][0, k])
    for l in range(2):
        b = C_LN + l * 32
        cols[:, b:b + 8] = ch(inp["ln1_g"][l])
        cols[:, b + 8:b + 16] = ch(inp["ln1_b"][l])
        cols[:, b + 16:b + 24] = ch(inp["ln2_g"][l])
        cols[:, b + 24:b + 32] = ch(inp["ln2_b"][l])
    for k in range(4):
        cols[:, C_SCW + k * 8:C_SCW + k * 8 + 8] = ch(inp["ssm_conv_w"][0, k])
    cols[:, C_SCB:C_SCB + 8] = ch(inp["ssm_conv_b"][0])
    cols[:, C_SGUB:C_SGUB + 4] = np.asarray(inp["sgu_b"][0], f).T
    rows = np.zeros((128, NROWS), f)
    bc = lambda v: np.broadcast_to(np.asarray(v, f).reshape(1, -1), (128, np.asarray(v).size))
    rows[:, R_DTB:R_DTB + 8] = bc(inp["ssm_dt_bias"][0])
    rows[:, R_ALOG:R_ALOG + 8] = bc(inp["ssm_a_log"][0])
    rows[:, R_SSMD:R_SSMD + 8] = bc(inp["ssm_d"][0])
    rows[:, R_SGUG:R_SGUG + 512] = bc(inp["sgu_ln_g"][0])
    rows[:, R_SGUB:R_SGUB + 512] = bc(inp["sgu_ln_b"][0])
    rows[:, R_NORMW:R_NORMW + 512] = bc(inp["ssm_norm_w"][0])
    cst = np.zeros((128, NCONST), f)
    cst[:, C_ID:C_ID + 128] = np.eye(128, dtype=f)
    s_i = np.arange(128)[:, None]
    t_i = np.arange(128)[None, :]
    cst[:, C_TRI:C_TRI + 128] = (s_i <= t_i).astype(f)
    cst[:, C_ONESM:C_ONESM + 128] = 1.0 / D
    cst[:, C_ONES:C_ONES + 128] = 1.0
    nm = np.where(s_i[:, :] > np.arange(64)[None, :], NEG, 0.0).astype(f)
    cst[:, C_NEG:C_NEG + 512] = np.tile(nm, (1, 8))
    for gi in range(4):
        win = 2 << gi
        cst[:, C_INVC + gi * 16:C_INVC + gi * 16 + 16] = 1.0 / np.minimum(win, np.arange(16) + 1.0)
    return cols, rows, cst


def kernel(**inp):
    NTQ = int(os.environ.get("MK_NTQ", "4"))
    f = np.float32
    A = lambda a: np.ascontiguousarray(np.asarray(a, f))
    if NTQ not in _NC_CACHE:
        _NC_CACHE[NTQ] = K(NTQ).build()
    nc = _NC_CACHE[NTQ]
    LQ = NTQ * 512
    TP = 4 * LQ
    cols, rows, cst = _pack(inp)
    shared = {
        "w_in_even": A(inp["w_in_even"][0]), "w_out_even": A(inp["w_out_even"][0]),
        "w_in_odd": A(inp["w_in_odd"][0]), "w_out_odd": A(inp["w_out_odd"][0]),
        "pool_w": A(inp["pool_mix_w"][0]),
        "sgu_wT": A(np.transpose(np.asarray(inp["sgu_w"][0]), (0, 2, 1))),
        "cols": cols, "rows": rows, "consts": cst,
    }
    for l in range(2):
        shared["w_ff1_%d" % l] = A(inp["w_ff1"][l])
        shared["w_ff3_%d" % l] = A(inp["w_ff3"][l])
        shared["w_ff2_%d" % l] = A(inp["w_ff2"][l])
        shared["w_ple_%d" % l] = A(inp["w_ple"][l])
        shared["w_gate_%d" % l] = A(inp["w_ple_gate"][l])
    xp = np.asarray(inp["x_prompt"], f)
    pp = np.asarray(inp["p_prompt"], f)
    xs = np.asarray(inp["x_sample"], f)
    psm = np.asarray(inp["p_sample"], f)
    gen = np.zeros((128, 64), f)
    spec = np.zeros((128, 64), f)
    for gi in range(4):
        win = 2 << gi
        gen[:, gi * 16:gi * 16 + 16] = 1.0 / win
        spec[:, gi * 16:gi * 16 + 16] = 1.0 / np.minimum(win, np.arange(16) + 1.0)
    in_maps = []
    for c in range(8):
        b, q = c // 4, c % 4
        d = dict(shared)
        nreal = (q + 1) * LQ
        xT = np.zeros((D, TP), f)
        pT = np.zeros((2, 256, TP), f)
        xT[:, TP - nreal:] = xp[b, :nreal].T
        pT[:, :, TP - nreal:] = np.transpose(pp[:, b, :nreal], (0, 2, 1))
        d["xpT"], d["ppT"] = xT, pT
        k0 = (3 - q) * NTQ
        flg = np.zeros((128, 4 * NTQ), f)
        flg[:, k0 + 1:] = 1.0
        d["flg"] = flg
        invt = np.zeros((128, 256), f)
        for j in range(4):
            invt[:, j * 64:(j + 1) * 64] = spec if j * NTQ == k0 else gen
        d["invt"] = invt
        d["xsT"] = A(xs[c].T)
        d["psT"] = A(np.transpose(psm[:, c], (0, 2, 1)))
        d["st_pool"] = A(np.asarray(inp["state_pool"])[0, c].T)
        d["st_conv"] = A(np.asarray(inp["state_conv"])[0, c].T)
        d["st_sconv"] = A(np.asarray(inp["state_ssm_conv"])[0, c].T)
        d["st_ssm"] = A(np.transpose(np.asarray(inp["state_ssm"])[0, c], (2, 0, 1)).reshape(128, 512))
        in_maps.append(d)
    res = run_bass_kernel_spmd(nc, in_maps, core_ids=list(range(8))).results
    unh = lambda a: np.ascontiguousarray(np.transpose(np.asarray(a, f).reshape(128, 8, 64), (1, 2, 0)))
    y_prompt = np.zeros((2, SEQ, D), f)
    for c in range(8):
        b, q = c // 4, c % 4
        y_prompt[b, q * LQ:(q + 1) * LQ] = res[c]["ypT"].T
    fin = [3, 7]
    y_sample = np.stack([res[c]["ysT"].T for c in range(8)]).astype(f)
    pool_p = np.stack([res[c]["o_pool_p"].T for c in fin])[None].astype(f)
    pool_s = np.stack([res[c]["o_pool_s"].T for c in range(8)])[None].astype(f)
    conv_p = np.stack([res[c]["o_conv_p"].T for c in fin])[None].astype(f)
    conv_s = np.stack([res[c]["o_conv_s"].T for c in range(8)])[None].astype(f)
    sguv = np.stack([res[c]["o_sguv"] for c in range(8)])[None].astype(f)
    sconv_p = np.stack([res[c]["o_sconv_p"].T for c in fin])[None].astype(f)
    sconv_s = np.stack([res[c]["o_sconv_s"].T for c in range(8)])[None].astype(f)
    ssm_p = np.stack([unh(res[c]["o_ssm_p"]) for c in fin])[None].astype(f)
    ssm_s = np.stack([unh(res[c]["o_ssm_s"]) for c in range(8)])[None].astype(f)
    return (np.ascontiguousarray(y_prompt), np.ascontiguousarray(y_sample), pool_p, pool_s, conv_p, conv_s,
            sguv, sconv_p, sconv_s, ssm_p, ssm_s)
```
